# Optimizing a Trainium2 kernel written in Bass

```python
import math
import jax, jax.numpy as jnp
from jax import lax
import numpy as np

D_MODEL = 1024
BATCH = 8
SEQ = 4096
DEPTH = 2

MEM_LEN = 256
EPS = 1e-6
D_FF = 2816
CONV_WIDTH = 3
CONV_CH = D_MODEL // 2
SWA_HEADS = 8
SWA_KV_HEADS = 2
SWA_HEAD_DIM = 64
WINDOW = 128
MOBA_HEADS = 8
MOBA_KV_HEADS = 4
MOBA_HEAD_DIM = D_MODEL // MOBA_HEADS
MOBA_BLOCK = 256
MOBA_TOPK = 3
MOBA_Q_CHUNK = 16
REL_BUCKETS = 32
REL_MAX_DIST = 128
REL_HEADS = SWA_HEADS
XA_HEADS = 4
XA_HEAD_DIM = 128
XA_W = XA_HEADS * XA_HEAD_DIM

N_EVEN = (DEPTH + 1) // 2
N_ODD = DEPTH // 2
SWA_Q_W = SWA_HEADS * SWA_HEAD_DIM
SWA_KV_W = SWA_KV_HEADS * SWA_HEAD_DIM
EVEN_IN = 3 * CONV_CH + SWA_Q_W + 2 * SWA_KV_W
EVEN_MIX = CONV_CH + SWA_Q_W
MOBA_Q_W = MOBA_HEADS * MOBA_HEAD_DIM
MOBA_KV_W = MOBA_KV_HEADS * MOBA_HEAD_DIM
ODD_IN = MOBA_Q_W + 2 * MOBA_KV_W

kernel_name = 'hybrid_shortconv_swa_moba_macaron'


def rmsnorm(x, g):
    x32 = x.astype(jnp.float32)
    y = x32 * lax.rsqrt(jnp.mean(x32 * x32, axis=-1, keepdims=True) + EPS)
    return y.astype(x.dtype) * g


def swiglu(x, w_gate, w_up, w_down):
    return (jax.nn.silu(x @ w_gate) * (x @ w_up)) @ w_down


def rel_bucket(dist):
    n = jnp.maximum(dist, 0)
    max_exact = REL_BUCKETS // 2
    nf = jnp.maximum(n, 1).astype(jnp.float32)
    large = max_exact + (jnp.log(nf / max_exact) / math.log(REL_MAX_DIST / max_exact)
                         * (REL_BUCKETS - max_exact)).astype(jnp.int32)
    large = jnp.minimum(large, REL_BUCKETS - 1)
    return jnp.where(n < max_exact, n, large)


def short_conv_mixer(b_gate, c_gate, u, conv_w):
    v = c_gate * u
    S = v.shape[1]
    vp = jnp.pad(v, ((0, 0), (CONV_WIDTH - 1, 0), (0, 0)))
    conv = vp[:, 0:S] * conv_w[0]
    for j in range(1, CONV_WIDTH):
        conv = conv + vp[:, j:j + S] * conv_w[j]
    return b_gate * conv


def sliding_window_attention(q, k, v, sinks, table):
    B, S, H, dh = q.shape
    Hkv = k.shape[2]
    G = H // Hkv
    nb = S // WINDOW
    qb = q.reshape(B, nb, WINDOW, Hkv, G, dh)

    def with_prev(t):
        t = t.reshape(B, nb, WINDOW, Hkv, dh)
        prev = jnp.pad(t, ((0, 0), (1, 0), (0, 0), (0, 0), (0, 0)))[:, :-1]
        return jnp.concatenate([prev, t], axis=2)

    kc, vc = with_prev(k), with_prev(v)
    logits = jnp.einsum('bnqkgd,bnjkd->bnkgqj', qb, kc).astype(jnp.float32) * dh ** -0.5
    qi = jnp.arange(WINDOW)[:, None]
    kj = jnp.arange(2 * WINDOW)[None, :]
    dist = qi + WINDOW - kj
    band = (dist >= 0) & (dist < WINDOW)
    mask = band[None] & ((jnp.arange(nb)[:, None, None] > 0) | (kj >= WINDOW)[None])
    bias = jnp.moveaxis(table[rel_bucket(dist)], -1, 0).reshape(Hkv, G, WINDOW, 2 * WINDOW)
    logits = jnp.where(mask[None, :, None, None], logits + bias, -jnp.inf)
    sink = sinks.astype(jnp.float32).reshape(Hkv, G)[None, None, :, :, None, None]
    m = jnp.maximum(jnp.max(logits, axis=-1, keepdims=True), sink)
    p = jnp.exp(logits - m)
    p = p / (jnp.sum(p, axis=-1, keepdims=True) + jnp.exp(sink - m))
    out = jnp.einsum('bnkgqj,bnjkd->bnqkgd', p.astype(v.dtype), vc)
    return out.reshape(B, S, H * dh)


def moba_attention(q, k, v, table):
    B, S, H, dh = q.shape
    Hkv = k.shape[2]
    G = H // Hkv
    s_pad = -(-S // MOBA_BLOCK) * MOBA_BLOCK
    nblk = s_pad // MOBA_BLOCK
    pad = ((0, 0), (0, s_pad - S), (0, 0), (0, 0))
    kblk = jnp.pad(k, pad).reshape(B, nblk, MOBA_BLOCK, Hkv, dh).transpose(0, 3, 1, 2, 4)
    vblk = jnp.pad(v, pad).reshape(B, nblk, MOBA_BLOCK, Hkv, dh).transpose(0, 3, 1, 2, 4)
    kmean = jnp.mean(kblk.astype(jnp.float32), axis=3)
    nq = S // MOBA_Q_CHUNK
    qc = q.reshape(B, nq, MOBA_Q_CHUNK, Hkv, G, dh).transpose(1, 0, 3, 4, 2, 5)
    topk = min(MOBA_TOPK, nblk)
    n_sel = topk * MOBA_BLOCK
    scale = dh ** -0.5
    b_ix = jnp.arange(B)[:, None, None, None, None]
    k_ix = jnp.arange(Hkv)[None, :, None, None, None]
    tbl = table.T.reshape(Hkv, G, REL_BUCKETS)
    k6 = jnp.arange(Hkv)[None, :, None, None, None, None]
    g6 = jnp.arange(G)[None, None, :, None, None, None]
    blk_ids = jnp.arange(nblk)
    offs = jnp.arange(MOBA_BLOCK)

    def one_chunk(args):
        ci, qx = args
        t = ci * MOBA_Q_CHUNK + jnp.arange(MOBA_Q_CHUNK)
        own = (ci * MOBA_Q_CHUNK) // MOBA_BLOCK
        gate = jnp.einsum('bkgqd,bknd->bkgqn', qx.astype(jnp.float32), kmean)
        gate = jnp.where(blk_ids < own, gate, -jnp.inf)
        _, sel = lax.top_k(gate, topk)
        valid = sel < own
        ks = kblk[b_ix, k_ix, sel]
        vs = vblk[b_ix, k_ix, sel]
        dist_sel = t[:, None, None] - (sel[..., None] * MOBA_BLOCK + offs)
        l_sel = jnp.einsum('bkgqd,bkgqsnd->bkgqsn', qx, ks).astype(jnp.float32) * scale
        l_sel = jnp.where(valid[..., None], l_sel + tbl[k6, g6, rel_bucket(dist_sel)], -jnp.inf)
        k_own = lax.dynamic_index_in_dim(kblk, own, axis=2, keepdims=False)
        v_own = lax.dynamic_index_in_dim(vblk, own, axis=2, keepdims=False)
        dist_own = t[:, None] - (own * MOBA_BLOCK + offs)[None, :]
        l_own = jnp.einsum('bkgqd,bknd->bkgqn', qx, k_own).astype(jnp.float32) * scale
        l_own = jnp.where(dist_own >= 0, l_own + tbl[:, :, rel_bucket(dist_own)], -jnp.inf)
        logits = jnp.concatenate([l_sel.reshape(l_sel.shape[:4] + (n_sel,)), l_own], axis=-1)
        p = jax.nn.softmax(logits, axis=-1).astype(v.dtype)
        p_sel = p[..., :n_sel].reshape(l_sel.shape)
        return (jnp.einsum('bkgqsn,bkgqsnd->bkgqd', p_sel, vs)
                + jnp.einsum('bkgqn,bknd->bkgqd', p[..., n_sel:], v_own))

    out = lax.map(one_chunk, (jnp.arange(nq), qc))
    return out.transpose(1, 0, 4, 2, 3, 5).reshape(B, S, H * dh)


def even_mixer(h, w_in, conv_w, sinks, w_out, table):
    B, S, _ = h.shape
    z = h @ w_in
    c1 = CONV_CH
    c2 = 2 * CONV_CH
    c3 = 3 * CONV_CH
    c4 = c3 + SWA_Q_W
    c5 = c4 + SWA_KV_W
    b_gate, c_gate, u, q, k, v = jnp.split(z, [c1, c2, c3, c4, c5], axis=-1)
    ya = short_conv_mixer(b_gate, c_gate, u, conv_w)
    yb = sliding_window_attention(q.reshape(B, S, SWA_HEADS, SWA_HEAD_DIM),
                                  k.reshape(B, S, SWA_KV_HEADS, SWA_HEAD_DIM),
                                  v.reshape(B, S, SWA_KV_HEADS, SWA_HEAD_DIM), sinks, table)
    return jnp.concatenate([ya, yb], axis=-1) @ w_out


def odd_mixer(h, w_in, w_out, table):
    B, S, _ = h.shape
    z = h @ w_in
    q, k, v = jnp.split(z, [MOBA_Q_W, MOBA_Q_W + MOBA_KV_W], axis=-1)
    y = moba_attention(q.reshape(B, S, MOBA_HEADS, MOBA_HEAD_DIM),
                       k.reshape(B, S, MOBA_KV_HEADS, MOBA_HEAD_DIM),
                       v.reshape(B, S, MOBA_KV_HEADS, MOBA_HEAD_DIM), table)
    return y @ w_out


def memory_cross_attention(h, mem_n, w_q, w_kv, w_o):
    B, S, _ = h.shape
    M = mem_n.shape[1]
    q = (h @ w_q).reshape(B, S, XA_HEADS, XA_HEAD_DIM)
    kv = (mem_n @ w_kv).reshape(B, M, 2, XA_HEADS, XA_HEAD_DIM)
    k, v = kv[:, :, 0], kv[:, :, 1]
    logits = jnp.einsum('bshd,bmhd->bhsm', q, k).astype(jnp.float32) * XA_HEAD_DIM ** -0.5
    p = jax.nn.softmax(logits, axis=-1).astype(v.dtype)
    o = jnp.einsum('bhsm,bmhd->bshd', p, v).reshape(B, S, XA_W)
    return o @ w_o


def setup_inputs(seed: int = 0) -> dict:
    key = jax.random.key(seed)
    ks = jax.random.split(key, 32)
    f32 = jnp.float32

    def w(k, shape, fan_in):
        return jax.random.normal(k, shape, f32) * fan_in ** -0.5

    def gain(k, shape):
        return 1.0 + 0.02 * jax.random.normal(k, shape, f32)

    return {
        'x': jax.random.normal(ks[0], (BATCH, SEQ, D_MODEL), f32),
        'mem': jax.random.normal(ks[1], (BATCH, MEM_LEN, D_MODEL), f32),
        'ffn1_norm': gain(ks[2], (DEPTH, D_MODEL)),
        'ffn1_w_gate': w(ks[3], (DEPTH, D_MODEL, D_FF), D_MODEL),
        'ffn1_w_up': w(ks[4], (DEPTH, D_MODEL, D_FF), D_MODEL),
        'ffn1_w_down': w(ks[5], (DEPTH, D_FF, D_MODEL), D_FF),
        'mix_norm': gain(ks[6], (DEPTH, D_MODEL)),
        'ev_w_in': w(ks[7], (N_EVEN, D_MODEL, EVEN_IN), D_MODEL),
        'ev_conv_w': w(ks[8], (N_EVEN, CONV_WIDTH, CONV_CH), CONV_WIDTH),
        'ev_sinks': 0.5 * jax.random.normal(ks[9], (N_EVEN, SWA_HEADS), f32),
        'ev_w_out': w(ks[10], (N_EVEN, EVEN_MIX, D_MODEL), EVEN_MIX),
        'od_w_in': w(ks[11], (N_ODD, D_MODEL, ODD_IN), D_MODEL),
        'od_w_out': w(ks[12], (N_ODD, MOBA_Q_W, D_MODEL), MOBA_Q_W),
        'rel_bias': 0.2 * jax.random.normal(ks[13], (REL_BUCKETS, REL_HEADS), f32),
        'xa_norm': gain(ks[14], (DEPTH, D_MODEL)),
        'xa_w_q': w(ks[15], (DEPTH, D_MODEL, XA_W), D_MODEL),
        'xa_w_kv': w(ks[16], (DEPTH, D_MODEL, 2 * XA_W), D_MODEL),
        'xa_w_o': w(ks[17], (DEPTH, XA_W, D_MODEL), XA_W),
        'mem_norm': gain(ks[18], (D_MODEL,)),
        'ffn2_norm': gain(ks[19], (DEPTH, D_MODEL)),
        'ffn2_w_gate': w(ks[20], (DEPTH, D_MODEL, D_FF), D_MODEL),
        'ffn2_w_up': w(ks[21], (DEPTH, D_MODEL, D_FF), D_MODEL),
        'ffn2_w_down': w(ks[22], (DEPTH, D_FF, D_MODEL), D_FF),
        'final_norm': gain(ks[23], (D_MODEL,)),
    }


def reference(x, mem, ffn1_norm, ffn1_w_gate, ffn1_w_up, ffn1_w_down, mix_norm,
              ev_w_in, ev_conv_w, ev_sinks, ev_w_out, od_w_in, od_w_out, rel_bias,
              xa_norm, xa_w_q, xa_w_kv, xa_w_o, mem_norm,
              ffn2_norm, ffn2_w_gate, ffn2_w_up, ffn2_w_down, final_norm):
    mem_n = rmsnorm(mem, mem_norm)
    for l in range(DEPTH):
        x = x + 0.5 * swiglu(rmsnorm(x, ffn1_norm[l]), ffn1_w_gate[l], ffn1_w_up[l], ffn1_w_down[l])
        h = rmsnorm(x, mix_norm[l])
        if l % 2 == 0:
            i = l // 2
            x = x + even_mixer(h, ev_w_in[i], ev_conv_w[i], ev_sinks[i], ev_w_out[i], rel_bias)
        else:
            i = l // 2
            x = x + odd_mixer(h, od_w_in[i], od_w_out[i], rel_bias)
        x = x + memory_cross_attention(rmsnorm(x, xa_norm[l]), mem_n, xa_w_q[l], xa_w_kv[l], xa_w_o[l])
        x = x + 0.5 * swiglu(rmsnorm(x, ffn2_norm[l]), ffn2_w_gate[l], ffn2_w_up[l], ffn2_w_down[l])
    return rmsnorm(x, final_norm)
```

```python
import math
import numpy as np
import concourse.bass as bass
import concourse.mybir as mybir
from concourse.bass_utils import run_bass_kernel_spmd

F32 = mybir.dt.float32
BF = mybir.dt.bfloat16
AF = mybir.ActivationFunctionType
ALU = mybir.AluOpType
AX = mybir.AxisListType

NCORES = 8
S = 4096
D = 1024
TS = 1024
NST = S // TS
SUB = 512
NSUB = TS // SUB
DFF = 2816
NFC = DFF // 128
MEM = 256
NEGB = -30000.0
STG = 2048
NSTG = 3
NWP = 4
EPS = 1e-6

_DSZ = {F32: 4, BF: 2}
BLK = 256


class Op:
    __slots__ = ("eng", "fn", "rk", "wk", "waits", "signal", "sigval", "seq", "vc", "dsem", "dval", "xdeps")


class Prog:
    ENG = ("pe", "act", "dve", "pool", "sp")

    def __init__(self, dry):
        self.dry = dry
        self.ops = []
        self.cache = {}
        self.nops = 0

    def regions(self, ap):
        key = (ap.tensor.name, ap.offset, ap.ap)
        r = self.cache.get(key)
        if r is not None:
            return r
        dims = ap.ap
        esz = _DSZ.get(ap.dtype, 4)
        rowstep = dims[0][0]
        foff = ap.offset % rowstep if rowstep > 0 else ap.offset
        fd = list(dims[1:])
        if not fd:
            fd = [(1, 1)]
        last = fd[-1]
        outer = fd[:-1]
        runlen = (last[1] - 1) * abs(last[0]) + 1
        nout = 1
        for s_, c_ in outer:
            nout *= c_
        starts = [foff]
        if nout <= 64:
            for s_, c_ in outer:
                starts = [st + i * s_ for st in starts for i in range(c_)]
        else:
            ext = sum((c_ - 1) * abs(s_) for s_, c_ in outer)
            runlen = runlen + ext
        name = ap.tensor.name
        if name.startswith("ps") and name[2:].isdigit():
            r = tuple((name, b) for b in range(8))
            self.cache[key] = r
            return r
        blks = set()
        for st in starts:
            b0 = (st * esz) // BLK
            b1 = ((st + runlen) * esz - 1) // BLK
            for b in range(b0, b1 + 1):
                blks.add((name, b))
        r = tuple(blks)
        self.cache[key] = r
        return r

    def op(self, eng, fn, reads=(), writes=(), rk=(), wk=(), xdeps=()):
        self.nops += 1
        if self.dry:
            return None
        o = Op()
        o.eng = eng
        o.fn = fn
        rks = list(rk)
        for a in reads:
            rks.extend(self.regions(a))
        wks = list(wk)
        for a in writes:
            wks.extend(self.regions(a))
        o.rk = rks
        o.wk = wks
        o.signal = False
        o.xdeps = tuple(xdeps)
        self.ops.append(o)
        return len(self.ops) - 1

    def analyze(self, n_dma_sems):
        ops = self.ops
        last_w = {}
        readers = {}
        known = {e: {} for e in self.ENG}
        known_dma = {e: {} for e in self.ENG}
        seqc = {e: 0 for e in self.ENG}
        dma_last = [None] * n_dma_sems
        dma_cnt = [0] * n_dma_sems
        ndma = 0
        for i, o in enumerate(ops):
            e = o.eng
            is_dma = e == "sp"
            deps = set(o.xdeps)
            for k in o.rk:
                w = last_w.get(k)
                if w is not None:
                    deps.add(w)
            for k in o.wk:
                w = last_w.get(k)
                if w is not None and (ops[w].eng != e or e != "pe"):
                    deps.add(w)
                rd = readers.get(k)
                if rd:
                    for re_, ri in rd.items():
                        if re_ == "sp":
                            for r in ri:
                                deps.add(r)
                        else:
                            deps.add(ri)
            for k in o.rk:
                rd = readers.get(k)
                if rd is None:
                    rd = {}
                    readers[k] = rd
                if is_dma:
                    rd.setdefault("sp", []).append(i)
                else:
                    rd[e] = i
            for k in o.wk:
                last_w[k] = i
                readers[k] = None
            if is_dma:
                si = ndma % n_dma_sems
                ndma += 1
                if dma_last[si] is not None:
                    deps.add(dma_last[si])
                dma_last[si] = i
                dma_cnt[si] += 1
                o.dsem = si
                o.dval = 16 * dma_cnt[si]
            deps.discard(i)
            kn = known[e]
            kd = known_dma[e]
            waits = []
            for d in sorted(deps):
                od = ops[d]
                if od.eng == "sp":
                    if d in kd:
                        continue
                    kd[d] = 1
                    if len(kd) > 512:
                        for kk in list(kd.keys())[:256]:
                            del kd[kk]
                    waits.append(d)
                    od.signal = True
                else:
                    if kn.get(od.eng, -1) >= od.seq:
                        continue
                    waits.append(d)
                    od.signal = True
                    kn[od.eng] = od.seq
                if od.vc:
                    for ke, kv in od.vc:
                        if kn.get(ke, -1) < kv:
                            kn[ke] = kv
            best = {}
            fw = []
            for d in waits:
                od = ops[d]
                if od.eng == "sp":
                    fw.append(d)
                else:
                    if od.eng not in best or ops[best[od.eng]].seq < od.seq:
                        best[od.eng] = d
            fw.extend(best.values())
            o.waits = fw
            o.seq = seqc[e]
            seqc[e] += 1
            if not is_dma:
                o.vc = tuple(kn.items())
            else:
                o.vc = tuple(kn.items())
        cnt = {e: 0 for e in self.ENG}
        for o in ops:
            if o.eng != "sp":
                if o.signal:
                    cnt[o.eng] += 1
                    o.sigval = cnt[o.eng]
                else:
                    o.sigval = None

    def emit(self, nc, sems, dsems, block):
        ops = self.ops
        by = {e: [] for e in self.ENG}
        for o in ops:
            by[o.eng].append(o)

        def run(engname, eng):
            for o in by[engname]:
                for d in o.waits:
                    od = ops[d]
                    if od.eng == "sp":
                        eng.wait_ge(dsems[od.dsem], od.dval)
                    else:
                        eng.wait_ge(sems[od.eng], od.sigval)
                ins = o.fn(eng)
                if engname == "sp":
                    ins.then_inc(dsems[o.dsem], 16)
                elif o.signal:
                    ins.then_inc(sems[engname], 1)

        @block.tensor
        def _(e):
            run("pe", e)

        @block.scalar
        def _(e):
            run("act", e)

        @block.vector
        def _(e):
            run("dve", e)

        @block.gpsimd
        def _(e):
            run("pool", e)

        @block.sync
        def _(e):
            run("sp", e)
            final = {}
            for o in by["sp"]:
                final[o.dsem] = max(final.get(o.dsem, 0), o.dval)
            for si, v in final.items():
                e.wait_ge(dsems[si], v)


def rel_bucket_np(dist):
    n = np.maximum(dist, 0)
    max_exact = 16
    nf = np.maximum(n, 1).astype(np.float32)
    large = max_exact + (np.log(nf / np.float32(max_exact)) / np.float32(math.log(128 / max_exact))
                         * np.float32(32 - max_exact)).astype(np.int32)
    large = np.minimum(large, 31)
    return np.where(n < max_exact, n, large)


LM = 1152
LS = 384
LF = LM + LS


def host_consts():
    c = {}
    c["ident"] = np.eye(128, dtype=np.float32)
    oh = np.zeros((33, LF), np.float32)
    for j in range(LM):
        d = j - 511
        b = 32 if d < 0 else int(rel_bucket_np(np.array([d]))[0])
        oh[b, j] = 1.0
    for j in range(LS):
        d = j - 127
        b = 32 if (d < 0 or d >= 128) else int(rel_bucket_np(np.array([d]))[0])
        oh[b, LM + j] = 1.0
    c["ohv"] = oh
    sel = np.zeros((32, 32, 128), np.float32)
    for i in range(32):
        sel[i, i, :] = 1.0
    c["sel"] = sel.reshape(32, 32 * 128)
    neg = np.zeros((16, 16), np.float32)
    am = np.zeros((16, 16), np.float32)
    bm = np.zeros((16, 16), np.float32)
    for own in range(16):
        for b in range(16):
            neg[own, b] = -1e30 if b >= own else 0.0
            am[own, b] = -NEGB if b < own else 0.0
            bm[own, b] = 0.0 if b == own else NEGB
    c["tabs"] = np.concatenate([neg.reshape(1, -1), am.reshape(1, -1), bm.reshape(1, -1)], axis=1)
    return c


WSHAPES = [("ffn1_w_gate", (2, D, DFF)), ("ffn1_w_up", (2, D, DFF)), ("ffn1_w_down", (2, DFF, D)),
           ("ffn2_w_gate", (2, D, DFF)), ("ffn2_w_up", (2, D, DFF)), ("ffn2_w_down", (2, DFF, D)),
           ("ev_w_in", (1, D, 2304)), ("ev_w_out", (1, D, D)), ("od_w_in", (1, D, 2048)), ("od_w_out", (1, D, D)),
           ("xa_w_q", (2, D, 512)), ("xa_w_kv", (2, D, D)), ("xa_w_o", (2, 512, D))]
WOFF = {}
_o = 0
for _n, _s in WSHAPES:
    WOFF[_n] = (_o, _s)
    _o += _s[0] * _s[1] * _s[2]
WTOT = _o
CSIZES = [("gl", 128 * 80), ("cw", 128 * 12), ("sinks", 8), ("relb", 256), ("ident", 128 * 128),
          ("ohv", 33 * LF), ("sel", 32 * 32 * 128), ("tabs", 768)]
COFF = {}
_o = 0
for _n, _s in CSIZES:
    COFF[_n] = (_o, _s)
    _o += _s
CTOT = _o


def build(nph=9):
    nc = bass.Bass("TRN2", target_bir_lowering=False)
    dt_in = {}

    def din(name, shape, dtype=F32):
        t = nc.dram_tensor(name, list(shape), dtype, kind="ExternalInput").ap()
        dt_in[name] = t
        return t

    xTd = din("xT", [D, S])
    memTd = din("memT", [D, MEM])
    cpk = din("cpack", [CTOT])
    wpk = din("wpack", [WTOT])

    def cview(name, pat, **kw):
        o, n = COFF[name]
        return cpk[o:o + n].rearrange(pat, **kw)

    gl_d = cview("gl", "(p n) -> p n", p=128)
    cw_d = cview("cw", "(p n) -> p n", p=128)
    sinks_d = cview("sinks", "(a n) -> a n", a=1)
    relb_d = cview("relb", "(b h) -> b h", h=8)
    ident_d = cview("ident", "(p n) -> p n", p=128)
    ohv_d = cview("ohv", "(p n) -> p n", p=33)
    sel_d = cview("sel", "(p n) -> p n", p=32)
    tabs_d = cview("tabs", "(a n) -> a n", a=1)

    def wview(name):
        o, (L_, R_, C_) = WOFF[name]
        return wpk[o:o + L_ * R_ * C_].rearrange("(l r c) -> l r c", l=L_, r=R_)

    w_f1g = wview("ffn1_w_gate"); w_f1u = wview("ffn1_w_up"); w_f1d = wview("ffn1_w_down")
    w_f2g = wview("ffn2_w_gate"); w_f2u = wview("ffn2_w_up"); w_f2d = wview("ffn2_w_down")
    w_evin = wview("ev_w_in"); w_evout = wview("ev_w_out")
    w_odin = wview("od_w_in"); w_odout = wview("od_w_out")
    w_xq = wview("xa_w_q"); w_xkv = wview("xa_w_kv"); w_xo = wview("xa_w_o")
    outTd = nc.dram_tensor("outT", [D, S], F32, kind="ExternalOutput").ap()
    frep_d = nc.dram_tensor("frep", [128, 8, LF], F32, kind="Internal").ap()
    kc_d = nc.dram_tensor("kcache", [4, 128, S], BF, kind="Internal").ap()
    vc_d = nc.dram_tensor("vcache", [4, S, 128], BF, kind="Internal").ap()
    wbf_d = nc.dram_tensor("wbf", [192, 128, STG], BF, kind="Internal").ap()

    from contextlib import ExitStack
    es = ExitStack()

    def sb(name, shape, dtype):
        return es.enter_context(nc.sbuf_tensor(name, list(shape), dtype))

    xT = sb("xTs", [128, 8, TS], F32)
    hT = sb("hTs", [128, 8, TS], BF)
    stage = sb("stage", [128, NSTG, STG], F32)
    WP = sb("WP", [128, NWP, STG], BF)
    ones_bf = sb("ones_bf", [128, 128], BF)
    ident_f = sb("ident_f", [128, 128], F32)
    ident_bf = sb("ident_bf", [128, 128], BF)
    gl = sb("gl_s", [128, 10, 8], F32)
    cw = sb("cw_s", [128, 3, 4], F32)
    tbl_bc = sb("tbl_bc", [128, 256], F32)
    sexp = sb("sexp", [128, 8], F32)
    tabs = sb("tabs_s", [128, 3, 16, 16], F32)
    sel = sb("sel_s", [128, 32, 128], BF)
    strips = sb("strips", [128, 8, 1024], BF)
    swaB = sb("swaB", [128, 8, 256], BF)
    memK = sb("memK", [128, 2, 4, MEM], BF)
    memV = sb("memV", [128, 2, 2, 512], BF)
    kmean = sb("kmean", [128, 4, 16], F32)
    ksum = sb("ksum", [128, 4, 4], F32)
    kmh = sb("kmh", [128, 4, 16], BF)
    kml = sb("kml", [128, 4, 16], BF)
    sq = sb("sq", [128, 2, SUB], BF)
    rt = sb("rt", [128, 1, SUB], F32)
    vhist = sb("vhist", [128, 4, 2], F32)
    skh = sb("skh", [64, 2, 128], BF)
    svh = sb("svh", [128, 128], BF)
    Pb = sb("Pb", [128, 3, SUB], BF)
    rD = sb("rD", [128, 1, SUB], F32)
    ARENA = 32 * 1024 + 256
    arena = sb("arena", [128, ARENA], BF)
    arena_f = arena.bitcast(F32) if hasattr(arena, "bitcast") else None

    ps = [es.enter_context(nc.psum_tensor(f"ps{i}", [128, 512], F32)) for i in range(8)]
    ps7b = ps[7].bitcast(BF)
    mb16 = sb("mb16", [128, 128], BF)

    sems = {e: es.enter_context(nc.semaphore(f"sem_{e}")) for e in ("pe", "act", "dve", "pool")}
    NDS = 24
    dsems = [es.enter_context(nc.semaphore(f"dsem{i}")) for i in range(NDS)]

    def abf(off, shape):
        n = int(np.prod(shape))
        v = arena[:, off:off + n]
        if len(shape) == 1:
            return v
        names = " ".join(f"d{i}" for i in range(len(shape)))
        kw = {f"d{i}": shape[i] for i in range(len(shape))}
        return v.rearrange(f"p ({names}) -> p {names}", **kw)

    def af32(off_bf, shape):
        n = int(np.prod(shape))
        o = off_bf // 2
        v = arena_f[:, o:o + n]
        if len(shape) == 1:
            return v
        names = " ".join(f"d{i}" for i in range(len(shape)))
        kw = {f"d{i}": shape[i] for i in range(len(shape))}
        return v.rearrange(f"p ({names}) -> p {names}", **kw)

    requests = []

    def record(P, req_list):
        st = {"req": 0, "loaded": 0, "stg": 0, "wp": 0, "ppb": 0, "sq": 0, "pb": 0, "rd": 0}

        def dma(out, in_, reads=(), writes=(), rk=(), wk=()):
            return P.op("sp", lambda e: e.dma_start(out=out, in_=in_), reads=reads, writes=writes, rk=rk, wk=wk)

        def stage_view(slot, shape, npart=128):
            n = int(np.prod(shape))
            v = stage[0:npart, slot, 0:n]
            if len(shape) == 1:
                return v
            names = " ".join(f"d{i}" for i in range(len(shape)))
            kw = {f"d{i}": shape[i] for i in range(len(shape))}
            return v.rearrange(f"p ({names}) -> p {names}", **kw)

        def issue_load(j):
            src, shape, npart = req_list[j]
            slot = j % NSTG
            sv = stage_view(slot, shape, npart)
            dma(sv, src, writes=[sv])

        def wload(src, shape, dst, eng=None, npart=128):
            stl = st.get("stile")
            n_ = int(np.prod(shape))
            if stl is not None:
                pid = st["pid"]
                st["pid"] += 1
                wv = wbf_d[pid, 0:npart, 0:n_]
                if len(shape) > 1:
                    names = " ".join(f"d{i}" for i in range(len(shape)))
                    kw = {f"d{i}": shape[i] for i in range(len(shape))}
                    wv = wv.rearrange(f"p ({names}) -> p {names}", **kw)
                if stl >= 1:
                    if P.dry:
                        P.nops += 1
                        return
                    dma(dst, wv, writes=[dst], rk=[("wbf", pid)])
                    return
            j = st["req"]
            st["req"] += 1
            if P.dry:
                requests.append((src, tuple(shape), npart))
                P.nops += 2
                return
            while st["loaded"] < min(len(req_list), j + NSTG):
                issue_load(st["loaded"])
                st["loaded"] += 1
            sv = stage_view(j % NSTG, shape, npart)
            ceng = eng or ("act", "pool", "act")[j % 3]
            cp(ceng, dst, sv)
            if stl is not None:
                dma(wv, dst, reads=[dst], wk=[("wbf", pid)])

        def wp_slot(shape, npart=128):
            i = st["wp"] % NWP
            st["wp"] += 1
            n = int(np.prod(shape))
            v = WP[0:npart, i, 0:n]
            names = " ".join(f"d{k}" for k in range(len(shape)))
            kw = {f"d{k}": shape[k] for k in range(len(shape))}
            return v.rearrange(f"p ({names}) -> p {names}", **kw)

        def mm(out, lhsT, rhs, start, stop):
            P.op("pe", lambda e: e.matmul(out, lhsT=lhsT, rhs=rhs, start=start, stop=stop),
                 reads=[lhsT, rhs], writes=[out])

        def act(out, in_, func, bias=None, scale=None, reads_extra=(), accum_out=None):
            kw = {}
            if accum_out is not None:
                kw["accum_out"] = accum_out
            if bias is not None:
                kw["bias"] = bias
            if scale is not None:
                kw["scale"] = scale
            rd = [in_] + list(reads_extra)
            if bias is not None and not isinstance(bias, float):
                rd.append(bias)
            if scale is not None and not isinstance(scale, float):
                rd.append(scale)
            P.op("act", lambda e: e.activation(out=out, in_=in_, func=func, **kw), reads=rd,
                 writes=[out] if accum_out is None else [out, accum_out])

        def tt(eng, out, in0, in1, op):
            P.op(eng, lambda e: e.tensor_tensor(out=out, in0=in0, in1=in1, op=op), reads=[in0, in1], writes=[out])

        def stt(out, in0, scalar, in1, op0, op1):
            rd = [in0, in1]
            if not isinstance(scalar, float):
                rd.append(scalar)
            P.op("dve", lambda e: e.scalar_tensor_tensor(out=out, in0=in0, scalar=scalar, in1=in1, op0=op0, op1=op1),
                 reads=rd, writes=[out])

        def ts(eng, out, in0, s1, s2, op0, op1=None):
            rd = [in0]
            if not isinstance(s1, float):
                rd.append(s1)
            if s2 is not None and not isinstance(s2, float):
                rd.append(s2)
            if op1 is None:
                P.op(eng, lambda e: e.tensor_scalar(out=out, in0=in0, scalar1=s1, scalar2=None, op0=op0), reads=rd, writes=[out])
            else:
                P.op(eng, lambda e: e.tensor_scalar(out=out, in0=in0, scalar1=s1, scalar2=s2, op0=op0, op1=op1), reads=rd, writes=[out])

        def cp(eng, out, in_):
            if eng == "act":
                P.op(eng, lambda e: e.activation(out=out, in_=in_, func=AF.Copy), reads=[in_], writes=[out])
            else:
                P.op(eng, lambda e: e.tensor_copy(out=out, in_=in_), reads=[in_], writes=[out])

        def recip(out, in_):
            P.op("dve", lambda e: e.reciprocal(out=out, in_=in_), reads=[in_], writes=[out])

        def memset(eng, ap, val):
            P.op(eng, lambda e: e.memset(ap, val), writes=[ap])

        def bcast_free(ap2d, n):
            d = ap2d.ap
            return bass.AP(ap2d.tensor, ap2d.offset, [list(d[0]), [0, n]])

        def setup():
            memset("pool", ones_bf[:, :], 1.0)
            memset("pool", kmean[:, :, :], 0.0)
            memset("pool", kmh[:, :, :], 0.0)
            memset("pool", kml[:, :, :], 0.0)
            dma(ident_f[:, :], ident_d[:, :], writes=[ident_f[:, :]])
            cp("pool", ident_bf[:, :], ident_f[:, :])
            dma(gl[:, :, :], gl_d.rearrange("p (n k) -> p n k", k=8), writes=[gl[:, :, :]])
            dma(cw[:, :, :], cw_d.rearrange("p (j c) -> p j c", c=4), writes=[cw[:, :, :]])
            dma(tbl_bc[:, :], relb_d.rearrange("b h -> (b h)").partition_broadcast(128), writes=[tbl_bc[:, :]])
            dma(sexp[:, :], sinks_d[0, :].partition_broadcast(128), writes=[sexp[:, :]])
            act(sexp[:, :], sexp[:, :], AF.Exp)
            import os as _os
            if not _os.environ.get("SKIPC"):
                dma(tabs[:, :, :, :].rearrange("p a b c -> p (a b c)"), tabs_d[0, :].partition_broadcast(128), writes=[tabs[:, :, :, :]])
            memset("pool", sel[:, :, :], 0.0)
            for half in range(0 if _os.environ.get("SKIPC") else 2):
                sv = stage[0:32, half, 0:2048]
                dma(sv, sel_d[:, half * 2048:(half + 1) * 2048], writes=[sv])
                cp("pool", sel[0:32, half * 16:(half + 1) * 16, :].rearrange("p a b -> p (a b)"), sv)
            tb_aug = af32(0, [8])[0:33]
            ohv = af32(16, [LF])[0:33]
            tbh = af32(16 + 2 * LF, [8, 128])[0:33]
            memset("pool", tb_aug[32:33, :], NEGB)
            dma(tb_aug[0:32, :], relb_d[:, :], writes=[tb_aug[0:32, :]])
            dma(ohv, ohv_d[:, :], writes=[ohv])
            for h in range(8):
                cp("dve", tbh[:, h, :], bcast_free(tb_aug[:, h:h + 1], 128))
            for h in range(8):
                for cch in range(3):
                    mm(ps[cch][:, :], tbh[:, h, :], ohv[:, cch * 512:(cch + 1) * 512], True, True)
                sv = stage[:, h % NSTG, 0:LF]
                for cch in range(3):
                    cp("act" if cch == 1 else "dve", sv[:, cch * 512:(cch + 1) * 512], ps[cch][:, :])
                dma(frep_d[:, h, :], sv, reads=[sv], wk=[("frep", h)])
            for h in range(8):
                s1 = stage[:, h % NSTG, 0:1024]
                src = bass.AP(frep_d.tensor, h * LF + 127, [[8 * LF - 1, 128], [1, 1024]])
                dma(s1, src, writes=[s1], rk=[("frep", h)])
                cp("pool", strips[:, h, :], s1)
                s2 = stage[:, h % NSTG, 1024:1280]
                src2 = bass.AP(frep_d.tensor, h * LF + LM + 127, [[8 * LF - 1, 128], [1, 256]])
                dma(s2, src2, writes=[s2], rk=[("frep", h)])
                cp("pool", swaB[:, h, :], s2)

        def rmsnorm(src, gi, dst, ncols=SUB, nsub=NSUB):
            for sub in range(nsub):
                pn = ps[7]
                for k in range(8):
                    s_ = sq[:, st["sq"] % 2, 0:ncols]
                    st["sq"] += 1
                    act(s_, src(k, sub), AF.Square)
                    mm(pn[:, 0:ncols], ones_bf[:, :], s_, k == 0, k == 7)
                r_ = rt[:, 0, 0:ncols]
                act(r_, pn[:, 0:ncols], AF.Sqrt, bias=eps_ap, scale=1.0 / D)
                rs = r_
                recip(rs, r_)
                for k in range(8):
                    stt(dst(k, sub), src(k, sub), gl[:, gi, k:k + 1], rs, ALU.mult, ALU.mult)

        def xsrc(k, sub):
            return xT[:, k, sub * SUB:(sub + 1) * SUB]

        def hdst(k, sub):
            return hT[:, k, sub * SUB:(sub + 1) * SUB]

        WGU_OFF = 0
        WD_OFF = 8192
        ACT_OFF = WD_OFF + 11264
        SIL_OFF = ACT_OFF + 11264
        assert SIL_OFF + 2048 <= ARENA

        def ffn(wg, wu, wd, l, gi):
            rmsnorm(xsrc, gi, hdst)
            wgu = abf(WGU_OFF, [2, 2, 8, 256])
            wdb = abf(WD_OFF, [11, 1024])
            actb = abf(ACT_OFF, [11, TS])
            sil = af32(SIL_OFF, [2, SUB])
            cnt = 0
            for half in range(2):
                c0 = half * 11
                for pr in range(6):
                    nch = 2 if pr < 5 else 1
                    col = (c0 + 2 * pr) * 128
                    wcols = nch * 128
                    b = pr % 2
                    wload(wg[l].rearrange("(k p) c -> p k c", p=128)[:, :, col:col + wcols], [8, wcols],
                          wgu[:, b, 0, :, 0:wcols], eng="act")
                    wload(wu[l].rearrange("(k p) c -> p k c", p=128)[:, :, col:col + wcols], [8, wcols],
                          wgu[:, b, 1, :, 0:wcols], eng="act")
                    wload(wd[l][(c0 + 2 * pr) * 128:(c0 + 2 * pr + nch) * 128, :].rearrange("(c p) n -> p c n", p=128),
                          [nch, 1024], wdb[:, 2 * pr:2 * pr + nch, :], eng="pool")
                    for ci in range(nch):
                        c = 2 * pr + ci
                        for sub in range(NSUB):
                            pg = ps[cnt % 2]
                            pu = ps[2 + cnt % 2]
                            cnt += 1
                            for k in range(8):
                                mm(pg[:, :], wgu[:, b, 0, k, ci * 128:(ci + 1) * 128], hdst(k, sub), k == 0, k == 7)
                            for k in range(8):
                                mm(pu[:, :], wgu[:, b, 1, k, ci * 128:(ci + 1) * 128], hdst(k, sub), k == 0, k == 7)
                            sl = sil[:, cnt % 2, :]
                            act(sl, pg[:, :], AF.Silu)
                            tt("dve", actb[:, c, sub * SUB:(sub + 1) * SUB], sl, pu[:, :], ALU.mult)
                for m in range(8):
                    for sub in range(NSUB):
                        pd = ps[4 + (m * NSUB + sub) % 2]
                        for c in range(11):
                            mm(pd[:, :], wdb[:, c, m * 128:(m + 1) * 128], actb[:, c, sub * SUB:(sub + 1) * SUB], c == 0, c == 10)
                        xs = xsrc(m, sub)
                        stt(xs, pd[:, :], 0.5, xs, ALU.mult, ALU.add)

        def attn(qT, ktiles, M, N, out_ap, den_add=None):
            n = len(ktiles)
            pO = ps[2 + st["ppb"] % 2]
            pD = ps[4 + st["ppb"] % 2]
            st["ppb"] += 1
            pbs = []

            def s_stage(i):
                t = ktiles[i]
                pS = ps[(0, 1, 6)[i % 3]]
                ex = t.get("extra", [])
                mm(pS[:, 0:N], t["kT"], qT, True, len(ex) == 0)
                for j, (l_, r_) in enumerate(ex):
                    mm(pS[:, 0:N], l_, r_, False, j == len(ex) - 1)
                pb = Pb[:, st["pb"] % 3, 0:N]
                st["pb"] += 1
                act(pb, pS[:, 0:N], AF.Exp, bias=t.get("bias"))
                pbs.append(pb)

            def o_stage(i):
                t = ktiles[i]
                mm(pO[0:M, 0:N], t["v"], pbs[i], i == 0, i == n - 1)
                mm(pD[0:M, 0:N], ones_bf[:, 0:M], pbs[i], i == 0, i == n - 1)

            s_stage(0)
            if n > 1:
                s_stage(1)
            for i in range(n):
                if i + 2 < n:
                    s_stage(i + 2)
                o_stage(i)
            r_ = rD[0:M, 0, 0:N]
            st["rd"] += 1
            if den_add is not None:
                den_add(r_, pD[0:M, 0:N])
                recip(r_, r_)
            else:
                recip(r_, pD[0:M, 0:N])
            po_v = pO[0:M, 0:N]
            if len(out_ap.ap) == 3:
                a_, b_ = out_ap.ap[1][1], out_ap.ap[2][1]
                po_v = bass.AP(po_v.tensor, po_v.offset, [list(po_v.ap[0]), [b_, a_], [1, b_]])
                r_ = bass.AP(r_.tensor, r_.offset, [list(r_.ap[0]), [b_, a_], [1, b_]])
            tt("dve", out_ap, po_v, r_, ALU.mult)

        def proj(wsrc, col0, ncols, M, evac, nk=8, rhs=None, pcols=256):
            rhs = rhs or hdst
            blk = 0
            for pc0 in range(col0, col0 + ncols, pcols):
                pw = min(pcols, col0 + ncols - pc0)
                wp = wp_slot([nk, pw])
                wload(wsrc.rearrange("(k p) c -> p k c", p=128)[:, :, pc0:pc0 + pw], [nk, pw], wp)
                for b in range(pw // M):
                    for sub in range(NSUB):
                        pp = ps[4 + st["ppb"] % 4]
                        st["ppb"] += 1
                        for k in range(nk):
                            mm(pp[0:M, :], wp[:, k, b * M:(b + 1) * M], rhs(k, sub), k == 0, k == nk - 1)
                        evac(blk, sub, pp[0:M, :])
                    blk += 1

        def resid_add(m, sub, pp):
            xs = xsrc(m, sub)
            tt("dve", xs, pp, xs, ALU.add)

        def xa(l, gi):
            rmsnorm(xsrc, gi, hdst)
            qx = abf(0, [4, TS])
            sc = 128 ** -0.5

            def ev_q(blk, sub, pp):
                act(qx[:, blk, sub * SUB:(sub + 1) * SUB], pp, AF.Copy, scale=sc)
            proj(w_xq[l], 0, 512, 128, ev_q)
            for h in range(4):
                for sub in range(NSUB):
                    q = qx[:, h, sub * SUB:(sub + 1) * SUB]
                    kts = [dict(kT=memK[:, l, h, j * 128:(j + 1) * 128], v=memV[:, l, j, h * 128:(h + 1) * 128]) for j in range(2)]
                    attn(q, kts, 128, SUB, q)
            proj(w_xo[l], 0, 1024, 128, resid_add, nk=4, rhs=lambda k, sub: qx[:, k, sub * SUB:(sub + 1) * SUB], pcols=512)

        YA_OFF = 0
        QS_OFF = 4096
        SK_OFF = QS_OFF + 8192
        SV_OFF = SK_OFF + 2304
        VB_OFF = SV_OFF + 1152
        CT_OFF = VB_OFF + 4224
        assert CT_OFF + 6144 <= ARENA

        def even_mixer(stile, gi):
            rmsnorm(xsrc, gi, hdst)
            w = w_evin[0]
            yaT = abf(YA_OFF, [4, TS])
            qsf = abf(QS_OFF, [8, TS])
            skf = abf(SK_OFF, [2, 128 + TS])
            qs = qsf[0:64]
            sk = skf[0:64]
            memset("pool", qsf[64:128, :, :], 0.0)
            memset("pool", skf[64:128, :, :], 0.0)
            sv = abf(SV_OFF, [9, 128])
            vbuf = af32(VB_OFF, [4, 528])
            ctmp = af32(CT_OFF, [3, SUB])
            ctmp2 = af32(CT_OFF + 3072, [3, SUB])
            wsrc = w.rearrange("(k p) c -> p k c", p=128)
            if stile == 0:
                memset("pool", vbuf[:, :, :], 0.0)
            else:
                cp("pool", vbuf[:, :, 0:2], vhist[:, :, :])
                cp("pool", sk[:, :, 0:128], skh[:, :, :])
                cp("pool", sv[:, 0, :], svh[:, :])
            for jp in range(2):
                wps = []
                for part in range(3):
                    wp = wp_slot([8, 256])
                    c0 = part * 512 + jp * 256
                    wload(wsrc[:, :, c0:c0 + 256], [8, 256], wp)
                    wps.append(wp)
                for ji in range(2):
                    j = jp * 2 + ji
                    for sub in range(NSUB):
                        it_ = j * NSUB + sub
                        pB, pC, pU = ((ps[4], ps[5], ps[6]), (ps[0], ps[1], ps[2]))[it_ % 2]
                        ct_ = (ctmp, ctmp2)[it_ % 2]
                        for part, pp in enumerate((pB, pC, pU)):
                            for k in range(8):
                                mm(pp[:, :], wps[part][:, k, ji * 128:(ji + 1) * 128], hdst(k, sub), k == 0, k == 7)
                        u_sb, t1, t2 = ct_[:, 0, :], ct_[:, 1, :], ct_[:, 2, :]
                        act(u_sb, pU[:, :], AF.Copy)
                        tt("dve", vbuf[:, j, 2:2 + SUB], pC[:, :], u_sb, ALU.mult)
                        act(t1, vbuf[:, j, 0:SUB], AF.Copy, scale=cw[:, 0, j:j + 1])
                        stt(t2, vbuf[:, j, 1:1 + SUB], cw[:, 1, j:j + 1], t1, ALU.mult, ALU.add)
                        stt(t1, vbuf[:, j, 2:2 + SUB], cw[:, 2, j:j + 1], t2, ALU.mult, ALU.add)
                        tt("dve", yaT[:, j, sub * SUB:(sub + 1) * SUB], pB[:, :], t1, ALU.mult)
                        cp("act", vbuf[:, j, 0:2], vbuf[:, j, SUB:SUB + 2])
            def ev_q(blk, sub, pp):
                act(qs[:, blk, sub * SUB:(sub + 1) * SUB], pp, AF.Copy, scale=0.125)
            proj(w, 1536, 512, 64, ev_q)
            wp = wp_slot([8, 256])
            wload(wsrc[:, :, 2048:2304], [8, 256], wp)
            for kvh in range(2):
                for sub in range(NSUB):
                    pp = ps[4 + st["ppb"] % 4]
                    st["ppb"] += 1
                    for k in range(8):
                        mm(pp[0:64, :], wp[:, k, kvh * 64:(kvh + 1) * 64], hdst(k, sub), k == 0, k == 7)
                    cp("act", sk[:, kvh, 128 + sub * SUB:128 + (sub + 1) * SUB], pp[0:64, :])
            for tt_ in range(8):
                pp = ps[4 + st["ppb"] % 4]
                st["ppb"] += 1
                for k in range(8):
                    mm(pp[:, 0:128], hT[:, k, tt_ * 128:(tt_ + 1) * 128], wp[:, k, 128:256], k == 0, k == 7)
                cp("act", sv[:, 1 + tt_, :], pp[:, 0:128])
            for n in range(8):
                first = (stile == 0 and n == 0)
                for g in range(2):
                    base = qsf[:, 4 * g, n * 128:(n + 1) * 128]
                    q = bass.AP(base.tensor, base.offset, [list(base.ap[0]), [TS, 4], [1, 128]])
                    kts = []
                    for typ in ((1,) if first else (0, 1)):
                        kcol = n * 128 + typ * 128
                        bsrc = swaB[:, 4 * g, (1 - typ) * 128:(1 - typ) * 128 + 128]
                        brhs = bass.AP(bsrc.tensor, bsrc.offset, [list(bsrc.ap[0]), [256, 4], [1, 128]])
                        kts.append(dict(kT=skf[:, g, kcol:kcol + 128], v=sv[:, n + typ, g * 64:(g + 1) * 64],
                                        extra=[(ident_bf[:, :], brhs)]))

                    def den_add(r_, pD_, g=g):
                        o3 = bass.AP(r_.tensor, r_.offset, [list(r_.ap[0]), [128, 4], [1, 128]])
                        i3 = bass.AP(pD_.tensor, pD_.offset, [list(pD_.ap[0]), [128, 4], [1, 128]])
                        sx = sexp[0:64, 4 * g:4 * g + 4]
                        s3 = bass.AP(sx.tensor, sx.offset, [list(sx.ap[0]), [1, 4], [0, 128]])
                        tt("dve", o3, i3, s3, ALU.add)
                    qo = bass.AP(q.tensor, q.offset, [[q.ap[0][0], 64]] + [list(d_) for d_ in q.ap[1:]])
                    attn(q, kts, 64, 512, qo, den_add=den_add)
            cp("pool", vhist[:, :, :], vbuf[:, :, 0:2])
            cp("pool", skh[:, :, :], sk[:, :, TS:TS + 128])
            cp("pool", svh[:, :], sv[:, 8, :])
            wo = w_evout[0]
            for pc in range(4):
                wpa = wp_slot([4, 256])
                wload(wo[0:512, :].rearrange("(k p) c -> p k c", p=128)[:, :, pc * 256:(pc + 1) * 256], [4, 256], wpa)
                wpb = wp_slot([8, 256], npart=64)
                wload(wo[512:1024, :].rearrange("(h p) c -> p h c", p=64)[:, :, pc * 256:(pc + 1) * 256], [8, 256], wpb, npart=64)
                for b in range(2):
                    m = pc * 2 + b
                    for sub in range(NSUB):
                        pp = ps[4 + st["ppb"] % 4]
                        st["ppb"] += 1
                        for j in range(4):
                            mm(pp[:, :], wpa[:, j, b * 128:(b + 1) * 128], yaT[:, j, sub * SUB:(sub + 1) * SUB], j == 0, False)
                        for h in range(8):
                            wpbf = bass.AP(wpb.tensor, wpb.offset, [[wpb.ap[0][0], 128]] + [list(d_) for d_ in wpb.ap[1:]])
                            mm(pp[:, :], wpbf[:, h, b * 128:(b + 1) * 128], qsf[:, h, sub * SUB:(sub + 1) * SUB], False, h == 7)
                        resid_add(m, sub, pp[:, :])

        QM_OFF = 0
        KB_OFF = 8192
        VB2_OFF = KB_OFF + 4096
        MT_OFF = VB2_OFF + 4096
        TMP_OFF = MT_OFF + 4096
        GM_OFF = TMP_OFF + 1024
        KB2_OFF = GM_OFF + 1024
        VB3_OFF = KB2_OFF + 4096
        assert VB3_OFF + 4096 <= ARENA

        def odd_mixer(stile, gi):
            rmsnorm(xsrc, gi, hdst)
            t0 = stile * TS
            w = w_odin[0]
            import os as _os
            if _os.environ.get("ODD_WSWAP"):
                w = w_evin[0]
            wsrc = w.rearrange("(k p) c -> p k c", p=128)
            qm = abf(QM_OFF, [8, TS])
            kbs = [abf(KB_OFF, [S]), abf(KB2_OFF, [S])]
            vbs = [abf(VB2_OFF, [32, 128]), abf(VB3_OFF, [32, 128])]
            mTf = abf(MT_OFF, [4, TS])
            mT = mTf[0:32]
            memset("pool", mTf[32:64, :, :], 0.0)
            memset("pool", mTf[64:128, :, :], 0.0)
            tmpkv = abf(TMP_OFF, [2, SUB])
            gm = af32(GM_OFF, [8, 16])
            top8 = af32(GM_OFF + 256, [8, 8])
            mb = af32(GM_OFF + 512, [8, 16])
            sc = 128 ** -0.5

            def ev_q(blk, sub, pp):
                act(qm[:, blk, sub * SUB:(sub + 1) * SUB], pp, AF.Copy, scale=sc)
            import os as _os
            _lvl = int(_os.environ.get("ODD_LVL", "9"))
            if _lvl <= 0:
                return
            proj(w, 0, 1024, 128, ev_q)
            if _lvl <= 1:
                return

            def ev_k(blk, sub, pp):
                tk = tmpkv[:, (blk * NSUB + sub) % 2, :]
                for hb in range(2):
                    act(tk[:, hb * 256:(hb + 1) * 256], pp[:, hb * 256:(hb + 1) * 256], AF.Copy,
                        accum_out=ksum[:, blk, 2 * sub + hb:2 * sub + hb + 1])
                tok = t0 + sub * SUB
                dma(kc_d[blk, :, tok:tok + SUB], tk, reads=[tk], wk=[("kc", blk, stile)])
            proj(w, 1024, 512, 128, ev_k)
            b0 = t0 // 256
            ts("dve", kmean[:, :, b0:b0 + 4], ksum[:, :, :], 1.0 / 256, None, ALU.mult)
            cp("dve", kmh[:, :, b0:b0 + 4], kmean[:, :, b0:b0 + 4])
            tt("dve", kml[:, :, b0:b0 + 4], kmean[:, :, b0:b0 + 4], kmh[:, :, b0:b0 + 4], ALU.subtract)
            if _lvl <= 2:
                return
            for pc in range(2):
                wp = wp_slot([8, 256])
                wload(wsrc[:, :, 1536 + pc * 256:1536 + (pc + 1) * 256], [8, 256], wp)
                for tt_ in range(8):
                    pp = ps[4 + st["ppb"] % 4]
                    st["ppb"] += 1
                    for k in range(8):
                        mm(pp[:, 0:256], hT[:, k, tt_ * 128:(tt_ + 1) * 128], wp[:, k, :], k == 0, k == 7)
                    tv = tmpkv[:, tt_ % 2, 0:256]
                    cp("act", tv, pp[:, 0:256])
                    tok = t0 + tt_ * 128
                    dst = vc_d[2 * pc:2 * pc + 2, tok:tok + 128, :].rearrange("v t d -> t v d")
                    import os as _os
                    if not _os.environ.get("ODD_NODMA") and not _os.environ.get("ODD_NOVDMA"):
                        for v_ in range(2):
                            dma(vc_d[2 * pc + v_, tok:tok + 128, :], tv[:, v_ * 128:(v_ + 1) * 128], reads=[tv[:, v_ * 128:(v_ + 1) * 128]],
                                wk=[("vc", 2 * pc + v_, stile)])
            import os as _os
            _stop = int(_os.environ.get("ODD_STOP", "9"))
            if _stop <= 1:
                return
            for qt in range(8):
                own = (t0 + qt * 128) // 256
                pG = ps[6]
                for h in range(8):
                    kv = h // 2
                    mm(pG[:, h * 16:(h + 1) * 16], qm[:, h, qt * 128:(qt + 1) * 128], kmh[:, kv, :], True, False)
                    mm(pG[:, h * 16:(h + 1) * 16], qm[:, h, qt * 128:(qt + 1) * 128], kml[:, kv, :], False, True)
                negr = tabs[:, 0, own, :]
                neg3 = bass.AP(negr.tensor, negr.offset, [list(negr.ap[0]), [0, 8], [1, 16]])
                pg3 = pG[:, 0:128].rearrange("p (h b) -> p h b", b=16)
                tt("dve", gm[:, :, :], pg3, neg3, ALU.add)
                for h in range(8):
                    P.op("dve", lambda e, h=h: e.max(out=top8[:, h, :], in_=gm[:, h, :]), reads=[gm[:, h, :]], writes=[top8[:, h, :]])
                thr = top8[:, :, 2:3]
                thr3 = bass.AP(thr.tensor, thr.offset, [list(thr.ap[0]), [8, 8], [0, 16]])
                tt("dve", mb[:, :, :], gm[:, :, :], thr3, ALU.is_ge)
                ar = tabs[:, 1, own, :]
                a3 = bass.AP(ar.tensor, ar.offset, [list(ar.ap[0]), [0, 8], [1, 16]])
                br = tabs[:, 2, own, :]
                b3 = bass.AP(br.tensor, br.offset, [list(br.ap[0]), [0, 8], [1, 16]])
                tt("dve", mb[:, :, :], mb[:, :, :], a3, ALU.mult)
                tt("dve", mb[:, :, :], mb[:, :, :], b3, ALU.add)
                pT = ps7b
                mbf = mb16[:, :]
                cp("dve", mbf, mb[:, :, :].rearrange("p h b -> p (h b)"))
                for pr in range(4):
                    P.op("pe", lambda e, pr=pr: e.transpose(out=pT[0:32, pr * 128:(pr + 1) * 128], in_=mbf[:, pr * 32:(pr + 1) * 32], identity=ident_bf[:, :]),
                         reads=[mbf[:, pr * 32:(pr + 1) * 32], ident_bf[:, :]], writes=[pT[0:32, pr * 128:(pr + 1) * 128]])
                cp("act", mT[:, :, qt * 128:(qt + 1) * 128], pT[0:32, 0:512].rearrange("p (a q) -> p a q", q=128))
            if _stop <= 2:
                return
            for qsub in range(NSUB):
                nkt = (t0 + (qsub + 1) * SUB) // 128
                for kv in range(4):
                    kb = kbs[(qsub * 4 + kv) % 2]
                    vb = vbs[(qsub * 4 + kv) % 2]
                    ntok = nkt * 128
                    rkeys = [("kc", kv, s_) for s_ in range(stile + 1)]
                    dma(kb[:, 0:ntok], kc_d[kv, :, 0:ntok], writes=[kb[:, 0:ntok]], rk=rkeys)
                    rkeys = [("vc", kv, s_) for s_ in range(stile + 1)]
                    for c8 in range(0, nkt, 4):
                        c9 = min(nkt, c8 + 4)
                        dma(vb[:, c8:c9, :], vc_d[kv, c8 * 128:c9 * 128, :].rearrange("(t p) d -> p t d", p=128), writes=[vb[:, c8:c9, :]], rk=rkeys)
                    for h in (2 * kv, 2 * kv + 1):
                        q = qm[:, h, qsub * SUB:(qsub + 1) * SUB]
                        kts = []
                        for kt in range(nkt):
                            a = kt - (nkt - 4)
                            blkid = kt // 2
                            ex = [(sel[:, (h % 2) * 16 + blkid, :], mTf[:, h // 2, qsub * SUB:(qsub + 1) * SUB])]
                            bias = None
                            if a >= -1:
                                c0 = (3 - a) * 128
                                ex.append((ident_bf[:, :], strips[:, h, c0:c0 + SUB]))
                            else:
                                bias = tbl_bc[:, 31 * 8 + h:31 * 8 + h + 1]
                            kts.append(dict(kT=kb[:, kt * 128:(kt + 1) * 128], v=vb[:, kt, :], extra=ex, bias=bias))
                        attn(q, kts, 128, SUB, q)
            proj(w_odout[0], 0, 1024, 128, resid_add, rhs=lambda k, sub: qm[:, k, sub * SUB:(sub + 1) * SUB])

        def prologue():
            memT = af32(0, [8, MEM])
            memn = abf(4096, [8, MEM])
            dma(memT, memTd.rearrange("(k p) m -> p k m", p=128), writes=[memT])
            rmsnorm(lambda k, sub: memT[:, k, :], 9, lambda k, sub: memn[:, k, :], ncols=MEM, nsub=1)
            for l in range(2):
                wsrc = w_xkv[l].rearrange("(k p) c -> p k c", p=128)
                for pc in range(2):
                    wp = wp_slot([8, 256])
                    wload(wsrc[:, :, pc * 256:(pc + 1) * 256], [8, 256], wp)
                    for b in range(2):
                        h = pc * 2 + b
                        pp = ps[4 + st["ppb"] % 4]
                        st["ppb"] += 1
                        for k in range(8):
                            mm(pp[:, 0:MEM], wp[:, k, b * 128:(b + 1) * 128], memn[:, k, :], k == 0, k == 7)
                        cp("act", memK[:, l, h, :], pp[:, 0:MEM])
                for pc in range(2):
                    wp = wp_slot([8, 256])
                    wload(wsrc[:, :, 512 + pc * 256:512 + (pc + 1) * 256], [8, 256], wp)
                    for j in range(2):
                        pp = ps[4 + st["ppb"] % 4]
                        st["ppb"] += 1
                        for k in range(8):
                            mm(pp[:, 0:256], memn[:, k, j * 128:(j + 1) * 128], wp[:, k, :], k == 0, k == 7)
                        cp("act", memV[:, l, j, pc * 256:(pc + 1) * 256], pp[:, 0:256])

        eps_ap = gl[:, 9, 0:1]
        eps_t = epsc[:, 0:1]
        eps_ap = eps_t
        memset("pool", epsc[:, :], EPS)
        setup()
        prologue()
        import os as _os2
        for stile in range(int(_os2.environ.get('NST_RUN', NST))):
            t0 = stile * TS
            st["stile"] = stile
            st["pid"] = 0
            nst_run = int(_os2.environ.get('NST_RUN', NST))
            xnext = af32(16384, [8, TS])
            if stile == 0 or nph < 9:
                dma(xT[:, :, :], xTd.rearrange("(k p) t -> p k t", p=128)[:, :, t0:t0 + TS], writes=[xT[:, :, :]])
            else:
                for k_ in range(8):
                    cp(("act", "dve")[k_ % 2], xT[:, k_, :], xnext[:, k_, :])
            ph = 0
            for l in range(2):
                if ph < nph:
                    ffn(w_f1g, w_f1u, w_f1d, l, 4 * l + 0)
                ph += 1
                if ph < nph:
                    if l == 0:
                        even_mixer(stile, 4 * l + 1)
                    else:
                        odd_mixer(stile, 4 * l + 1)
                ph += 1
                if ph < nph:
                    xa(l, 4 * l + 2)
                ph += 1
                if ph < nph:
                    ffn(w_f2g, w_f2u, w_f2d, l, 4 * l + 3)
                ph += 1
            if nph >= 9:
                ofin = af32(0, [8, SUB])
                if stile + 1 < nst_run:
                    dma(xnext, xTd.rearrange("(k p) t -> p k t", p=128)[:, :, t0 + TS:t0 + 2 * TS], writes=[xnext])
                for sub in range(NSUB):
                    rmsnorm(lambda k, s_: xsrc(k, sub), 8, lambda k, s_: ofin[:, k, :], nsub=1)
                    dma(outTd.rearrange("(k p) t -> p k t", p=128)[:, :, t0 + sub * SUB:t0 + (sub + 1) * SUB], ofin, reads=[ofin])
            else:
                dma(outTd.rearrange("(k p) t -> p k t", p=128)[:, :, t0:t0 + TS], xT[:, :, :], reads=[xT[:, :, :]])

    epsc = sb("epsc", [128, 8], F32)
    Pd = Prog(dry=True)
    record(Pd, None)
    P = Prog(dry=False)
    record(P, list(requests))
    P.analyze(NDS)
    with nc.Block() as block:
        P.emit(nc, sems, dsems, block)
    es.close()
    return nc, len(P.ops)


_CONSTS = None


def make_in_maps(inputs):
    global _CONSTS
    if _CONSTS is None:
        _CONSTS = host_consts()
    c = _CONSTS
    f = lambda a: np.ascontiguousarray(np.asarray(a, dtype=np.float32))
    x = f(inputs["x"])
    mem = f(inputs["mem"])
    G = np.stack([f(inputs["ffn1_norm"])[0], f(inputs["mix_norm"])[0], f(inputs["xa_norm"])[0], f(inputs["ffn2_norm"])[0],
                  f(inputs["ffn1_norm"])[1], f(inputs["mix_norm"])[1], f(inputs["xa_norm"])[1], f(inputs["ffn2_norm"])[1],
                  f(inputs["final_norm"]), f(inputs["mem_norm"])], axis=0)
    gl = np.ascontiguousarray(G.reshape(10, 8, 128).transpose(2, 0, 1).reshape(128, 80))
    cwv = f(inputs["ev_conv_w"])[0]
    cw = np.ascontiguousarray(cwv.reshape(3, 4, 128).transpose(2, 0, 1).reshape(128, 12))
    cparts = {"gl": gl, "cw": cw, "sinks": f(inputs["ev_sinks"]).reshape(-1), "relb": f(inputs["rel_bias"]),
              "ident": c["ident"], "ohv": c["ohv"], "sel": c["sel"], "tabs": c["tabs"]}
    cpack = np.concatenate([np.asarray(cparts[n], np.float32).reshape(-1) for n, _ in CSIZES])
    assert cpack.size == CTOT
    wpack = np.concatenate([f(inputs[n]).reshape(-1) for n, _ in WSHAPES])
    assert wpack.size == WTOT
    shared = {"cpack": cpack, "wpack": wpack}
    maps = []
    for b in range(NCORES):
        m = dict(shared)
        m["xT"] = np.ascontiguousarray(x[b].T)
        m["memT"] = np.ascontiguousarray(mem[b].T)
        maps.append(m)
    return maps


_NC_CACHE = {}


def kernel(**inputs):
    if "nc" not in _NC_CACHE:
        _NC_CACHE["nc"] = build(9)[0]
    nc = _NC_CACHE["nc"]
    maps = make_in_maps(inputs)
    res = run_bass_kernel_spmd(nc, maps, core_ids=list(range(NCORES)))
    out = np.stack([np.ascontiguousarray(np.asarray(r["outT"]).T) for r in res.results], axis=0)
    return out.astype(np.float32)
```

```python
import math
import numpy as np
import concourse.bass as bass
import concourse.mybir as mybir
from concourse.bass_utils import run_bass_kernel_spmd

F32 = mybir.dt.float32
BF = mybir.dt.bfloat16
AF = mybir.ActivationFunctionType
ALU = mybir.AluOpType
AX = mybir.AxisListType

NCORES = 8
S = 4096
D = 1024
TS = 1024
NST = S // TS
SUB = 512
NSUB = TS // SUB
DFF = 2816
NFC = DFF // 128
MEM = 256
NEGB = -30000.0
STG = 2048
NSTG = 3
NWP = 4
EPS = 1e-6

_DSZ = {F32: 4, BF: 2}
BLK = 256


class Op:
    __slots__ = ("eng", "fn", "rk", "wk", "waits", "signal", "sigval", "seq", "vc", "dsem", "dval", "xdeps")


class Prog:
    ENG = ("pe", "act", "dve", "pool", "sp")

    def __init__(self, dry):
        self.dry = dry
        self.ops = []
        self.cache = {}
        self.nops = 0

    def regions(self, ap):
        key = (ap.tensor.name, ap.offset, ap.ap)
        r = self.cache.get(key)
        if r is not None:
            return r
        dims = ap.ap
        esz = _DSZ.get(ap.dtype, 4)
        rowstep = dims[0][0]
        foff = ap.offset % rowstep if rowstep > 0 else ap.offset
        fd = list(dims[1:])
        if not fd:
            fd = [(1, 1)]
        last = fd[-1]
        outer = fd[:-1]
        runlen = (last[1] - 1) * abs(last[0]) + 1
        nout = 1
        for s_, c_ in outer:
            nout *= c_
        starts = [foff]
        if nout <= 64:
            for s_, c_ in outer:
                starts = [st + i * s_ for st in starts for i in range(c_)]
        else:
            ext = sum((c_ - 1) * abs(s_) for s_, c_ in outer)
            runlen = runlen + ext
        name = ap.tensor.name
        if name.startswith("ps") and name[2:].isdigit():
            r = tuple((name, b) for b in range(8))
            self.cache[key] = r
            return r
        blks = set()
        for st in starts:
            b0 = (st * esz) // BLK
            b1 = ((st + runlen) * esz - 1) // BLK
            for b in range(b0, b1 + 1):
                blks.add((name, b))
        r = tuple(blks)
        self.cache[key] = r
        return r

    def op(self, eng, fn, reads=(), writes=(), rk=(), wk=(), xdeps=()):
        self.nops += 1
        if self.dry:
            return None
        o = Op()
        o.eng = eng
        o.fn = fn
        rks = list(rk)
        for a in reads:
            rks.extend(self.regions(a))
        wks = list(wk)
        for a in writes:
            wks.extend(self.regions(a))
        o.rk = rks
        o.wk = wks
        o.signal = False
        o.xdeps = tuple(xdeps)
        self.ops.append(o)
        return len(self.ops) - 1

    def analyze(self, n_dma_sems):
        ops = self.ops
        last_w = {}
        readers = {}
        known = {e: {} for e in self.ENG}
        known_dma = {e: {} for e in self.ENG}
        seqc = {e: 0 for e in self.ENG}
        dma_last = [None] * n_dma_sems
        dma_cnt = [0] * n_dma_sems
        ndma = 0
        for i, o in enumerate(ops):
            e = o.eng
            is_dma = e == "sp"
            deps = set(o.xdeps)
            for k in o.rk:
                w = last_w.get(k)
                if w is not None:
                    deps.add(w)
            for k in o.wk:
                w = last_w.get(k)
                if w is not None and (ops[w].eng != e or e != "pe"):
                    deps.add(w)
                rd = readers.get(k)
                if rd:
                    for re_, ri in rd.items():
                        if re_ == "sp":
                            for r in ri:
                                deps.add(r)
                        else:
                            deps.add(ri)
            for k in o.rk:
                rd = readers.get(k)
                if rd is None:
                    rd = {}
                    readers[k] = rd
                if is_dma:
                    rd.setdefault("sp", []).append(i)
                else:
                    rd[e] = i
            for k in o.wk:
                last_w[k] = i
                readers[k] = None
            if is_dma:
                si = ndma % n_dma_sems
                ndma += 1
                if dma_last[si] is not None:
                    deps.add(dma_last[si])
                dma_last[si] = i
                dma_cnt[si] += 1
                o.dsem = si
                o.dval = 16 * dma_cnt[si]
            deps.discard(i)
            kn = known[e]
            kd = known_dma[e]
            waits = []
            for d in sorted(deps):
                od = ops[d]
                if od.eng == "sp":
                    if d in kd:
                        continue
                    kd[d] = 1
                    if len(kd) > 512:
                        for kk in list(kd.keys())[:256]:
                            del kd[kk]
                    waits.append(d)
                    od.signal = True
                else:
                    if kn.get(od.eng, -1) >= od.seq:
                        continue
                    waits.append(d)
                    od.signal = True
                    kn[od.eng] = od.seq
                if od.vc:
                    for ke, kv in od.vc:
                        if kn.get(ke, -1) < kv:
                            kn[ke] = kv
            best = {}
            fw = []
            for d in waits:
                od = ops[d]
                if od.eng == "sp":
                    fw.append(d)
                else:
                    if od.eng not in best or ops[best[od.eng]].seq < od.seq:
                        best[od.eng] = d
            fw.extend(best.values())
            o.waits = fw
            o.seq = seqc[e]
            seqc[e] += 1
            if not is_dma:
                o.vc = tuple(kn.items())
            else:
                o.vc = tuple(kn.items())
        cnt = {e: 0 for e in self.ENG}
        for o in ops:
            if o.eng != "sp":
                if o.signal:
                    cnt[o.eng] += 1
                    o.sigval = cnt[o.eng]
                else:
                    o.sigval = None

    def emit(self, nc, sems, dsems, block):
        ops = self.ops
        by = {e: [] for e in self.ENG}
        for o in ops:
            by[o.eng].append(o)

        def run(engname, eng):
            for o in by[engname]:
                for d in o.waits:
                    od = ops[d]
                    if od.eng == "sp":
                        eng.wait_ge(dsems[od.dsem], od.dval)
                    else:
                        eng.wait_ge(sems[od.eng], od.sigval)
                ins = o.fn(eng)
                if engname == "sp":
                    ins.then_inc(dsems[o.dsem], 16)
                elif o.signal:
                    ins.then_inc(sems[engname], 1)

        @block.tensor
        def _(e):
            run("pe", e)

        @block.scalar
        def _(e):
            run("act", e)

        @block.vector
        def _(e):
            run("dve", e)

        @block.gpsimd
        def _(e):
            run("pool", e)

        @block.sync
        def _(e):
            run("sp", e)
            final = {}
            for o in by["sp"]:
                final[o.dsem] = max(final.get(o.dsem, 0), o.dval)
            for si, v in final.items():
                e.wait_ge(dsems[si], v)


def rel_bucket_np(dist):
    n = np.maximum(dist, 0)
    max_exact = 16
    nf = np.maximum(n, 1).astype(np.float32)
    large = max_exact + (np.log(nf / np.float32(max_exact)) / np.float32(math.log(128 / max_exact))
                         * np.float32(32 - max_exact)).astype(np.int32)
    large = np.minimum(large, 31)
    return np.where(n < max_exact, n, large)


LM = 1152
LS = 384
LF = LM + LS


def host_consts():
    c = {}
    c["ident"] = np.eye(128, dtype=np.float32)
    oh = np.zeros((33, LF), np.float32)
    for j in range(LM):
        d = j - 511
        b = 32 if d < 0 else int(rel_bucket_np(np.array([d]))[0])
        oh[b, j] = 1.0
    for j in range(LS):
        d = j - 127
        b = 32 if (d < 0 or d >= 128) else int(rel_bucket_np(np.array([d]))[0])
        oh[b, LM + j] = 1.0
    c["ohv"] = oh
    sel = np.zeros((32, 32, 128), np.float32)
    for i in range(32):
        sel[i, i, :] = 1.0
    c["sel"] = sel.reshape(32, 32 * 128)
    neg = np.zeros((16, 16), np.float32)
    am = np.zeros((16, 16), np.float32)
    bm = np.zeros((16, 16), np.float32)
    for own in range(16):
        for b in range(16):
            neg[own, b] = -1e30 if b >= own else 0.0
            am[own, b] = -NEGB if b < own else 0.0
            bm[own, b] = 0.0 if b == own else NEGB
    c["tabs"] = np.concatenate([neg.reshape(1, -1), am.reshape(1, -1), bm.reshape(1, -1)], axis=1)
    return c


WSHAPES = [("ffn1_w_gate", (2, D, DFF)), ("ffn1_w_up", (2, D, DFF)), ("ffn1_w_down", (2, DFF, D)),
           ("ffn2_w_gate", (2, D, DFF)), ("ffn2_w_up", (2, D, DFF)), ("ffn2_w_down", (2, DFF, D)),
           ("ev_w_in", (1, D, 2304)), ("ev_w_out", (1, D, D)), ("od_w_in", (1, D, 2048)), ("od_w_out", (1, D, D)),
           ("xa_w_q", (2, D, 512)), ("xa_w_kv", (2, D, D)), ("xa_w_o", (2, 512, D))]
WOFF = {}
_o = 0
for _n, _s in WSHAPES:
    WOFF[_n] = (_o, _s)
    _o += _s[0] * _s[1] * _s[2]
WTOT = _o
CSIZES = [("gl", 128 * 80), ("cw", 128 * 12), ("sinks", 8), ("relb", 256), ("ident", 128 * 128),
          ("ohv", 33 * LF), ("sel", 32 * 32 * 128), ("tabs", 768)]
COFF = {}
_o = 0
for _n, _s in CSIZES:
    COFF[_n] = (_o, _s)
    _o += _s
CTOT = _o


def build(nph=9):
    nc = bass.Bass("TRN2", target_bir_lowering=False)
    dt_in = {}

    def din(name, shape, dtype=F32):
        t = nc.dram_tensor(name, list(shape), dtype, kind="ExternalInput").ap()
        dt_in[name] = t
        return t

    xTd = din("xT", [D, S])
    memTd = din("memT", [D, MEM])
    cpk = din("cpack", [CTOT])
    wpk = din("wpack", [WTOT])

    def cview(name, pat, **kw):
        o, n = COFF[name]
        return cpk[o:o + n].rearrange(pat, **kw)

    gl_d = cview("gl", "(p n) -> p n", p=128)
    cw_d = cview("cw", "(p n) -> p n", p=128)
    sinks_d = cview("sinks", "(a n) -> a n", a=1)
    relb_d = cview("relb", "(b h) -> b h", h=8)
    ident_d = cview("ident", "(p n) -> p n", p=128)
    ohv_d = cview("ohv", "(p n) -> p n", p=33)
    sel_d = cview("sel", "(p n) -> p n", p=32)
    tabs_d = cview("tabs", "(a n) -> a n", a=1)

    def wview(name):
        o, (L_, R_, C_) = WOFF[name]
        return wpk[o:o + L_ * R_ * C_].rearrange("(l r c) -> l r c", l=L_, r=R_)

    w_f1g = wview("ffn1_w_gate"); w_f1u = wview("ffn1_w_up"); w_f1d = wview("ffn1_w_down")
    w_f2g = wview("ffn2_w_gate"); w_f2u = wview("ffn2_w_up"); w_f2d = wview("ffn2_w_down")
    w_evin = wview("ev_w_in"); w_evout = wview("ev_w_out")
    w_odin = wview("od_w_in"); w_odout = wview("od_w_out")
    w_xq = wview("xa_w_q"); w_xkv = wview("xa_w_kv"); w_xo = wview("xa_w_o")
    outTd = nc.dram_tensor("outT", [D, S], F32, kind="ExternalOutput").ap()
    frep_d = nc.dram_tensor("frep", [128, 8, LF], F32, kind="Internal").ap()
    kc_d = nc.dram_tensor("kcache", [4, 128, S], BF, kind="Internal").ap()
    vc_d = nc.dram_tensor("vcache", [4, S, 128], BF, kind="Internal").ap()
    wbf_d = nc.dram_tensor("wbf", [192, 128, STG], BF, kind="Internal").ap()

    from contextlib import ExitStack
    es = ExitStack()

    def sb(name, shape, dtype):
        return es.enter_context(nc.sbuf_tensor(name, list(shape), dtype))

    xT = sb("xTs", [128, 8, TS], F32)
    hT = sb("hTs", [128, 8, TS], BF)
    stage = sb("stage", [128, NSTG, STG], F32)
    WP = sb("WP", [128, NWP, STG], BF)
    ones_bf = sb("ones_bf", [128, 128], BF)
    ident_f = sb("ident_f", [128, 128], F32)
    ident_bf = sb("ident_bf", [128, 128], BF)
    gl = sb("gl_s", [128, 10, 8], F32)
    cw = sb("cw_s", [128, 3, 4], F32)
    tbl_bc = sb("tbl_bc", [128, 256], F32)
    sexp = sb("sexp", [128, 8], F32)
    tabs = sb("tabs_s", [128, 3, 16, 16], F32)
    sel = sb("sel_s", [128, 32, 128], BF)
    strips = sb("strips", [128, 8, 1024], BF)
    swaB = sb("swaB", [128, 8, 256], BF)
    memK = sb("memK", [128, 2, 4, MEM], BF)
    memV = sb("memV", [128, 2, 2, 512], BF)
    kmean = sb("kmean", [128, 4, 16], F32)
    ksum = sb("ksum", [128, 4, 4], F32)
    kmh = sb("kmh", [128, 4, 16], BF)
    kml = sb("kml", [128, 4, 16], BF)
    sq = sb("sq", [128, 2, SUB], BF)
    rt = sb("rt", [128, 1, SUB], F32)
    vhist = sb("vhist", [128, 4, 2], F32)
    skh = sb("skh", [64, 2, 128], BF)
    svh = sb("svh", [128, 128], BF)
    Pb = sb("Pb", [128, 3, SUB], BF)
    rD = sb("rD", [128, 1, SUB], F32)
    ARENA = 32 * 1024 + 256
    arena = sb("arena", [128, ARENA], BF)
    arena_f = arena.bitcast(F32) if hasattr(arena, "bitcast") else None

    ps = [es.enter_context(nc.psum_tensor(f"ps{i}", [128, 512], F32)) for i in range(8)]
    ps7b = ps[7].bitcast(BF)
    mb16 = sb("mb16", [128, 128], BF)

    sems = {e: es.enter_context(nc.semaphore(f"sem_{e}")) for e in ("pe", "act", "dve", "pool")}
    NDS = 24
    dsems = [es.enter_context(nc.semaphore(f"dsem{i}")) for i in range(NDS)]

    def abf(off, shape):
        n = int(np.prod(shape))
        v = arena[:, off:off + n]
        if len(shape) == 1:
            return v
        names = " ".join(f"d{i}" for i in range(len(shape)))
        kw = {f"d{i}": shape[i] for i in range(len(shape))}
        return v.rearrange(f"p ({names}) -> p {names}", **kw)

    def af32(off_bf, shape):
        n = int(np.prod(shape))
        o = off_bf // 2
        v = arena_f[:, o:o + n]
        if len(shape) == 1:
            return v
        names = " ".join(f"d{i}" for i in range(len(shape)))
        kw = {f"d{i}": shape[i] for i in range(len(shape))}
        return v.rearrange(f"p ({names}) -> p {names}", **kw)

    requests = []

    def record(P, req_list):
        st = {"req": 0, "loaded": 0, "stg": 0, "wp": 0, "ppb": 0, "sq": 0, "pb": 0, "rd": 0}

        def dma(out, in_, reads=(), writes=(), rk=(), wk=()):
            return P.op("sp", lambda e: e.dma_start(out=out, in_=in_), reads=reads, writes=writes, rk=rk, wk=wk)

        def stage_view(slot, shape, npart=128):
            n = int(np.prod(shape))
            v = stage[0:npart, slot, 0:n]
            if len(shape) == 1:
                return v
            names = " ".join(f"d{i}" for i in range(len(shape)))
            kw = {f"d{i}": shape[i] for i in range(len(shape))}
            return v.rearrange(f"p ({names}) -> p {names}", **kw)

        def issue_load(j):
            src, shape, npart = req_list[j]
            slot = j % NSTG
            sv = stage_view(slot, shape, npart)
            dma(sv, src, writes=[sv])

        def wload(src, shape, dst, eng=None, npart=128):
            stl = st.get("stile")
            n_ = int(np.prod(shape))
            if stl is not None:
                pid = st["pid"]
                st["pid"] += 1
                wv = wbf_d[pid, 0:npart, 0:n_]
                if len(shape) > 1:
                    names = " ".join(f"d{i}" for i in range(len(shape)))
                    kw = {f"d{i}": shape[i] for i in range(len(shape))}
                    wv = wv.rearrange(f"p ({names}) -> p {names}", **kw)
                if stl >= 1:
                    if P.dry:
                        P.nops += 1
                        return
                    dma(dst, wv, writes=[dst], rk=[("wbf", pid)])
                    return
            j = st["req"]
            st["req"] += 1
            if P.dry:
                requests.append((src, tuple(shape), npart))
                P.nops += 2
                return
            while st["loaded"] < min(len(req_list), j + NSTG):
                issue_load(st["loaded"])
                st["loaded"] += 1
            sv = stage_view(j % NSTG, shape, npart)
            ceng = eng or "act"
            cp(ceng, dst, sv)
            if stl is not None:
                dma(wv, dst, reads=[dst], wk=[("wbf", pid)])

        def wp_slot(shape, npart=128):
            i = st["wp"] % NWP
            st["wp"] += 1
            n = int(np.prod(shape))
            v = WP[0:npart, i, 0:n]
            names = " ".join(f"d{k}" for k in range(len(shape)))
            kw = {f"d{k}": shape[k] for k in range(len(shape))}
            return v.rearrange(f"p ({names}) -> p {names}", **kw)

        def mm(out, lhsT, rhs, start, stop):
            P.op("pe", lambda e: e.matmul(out, lhsT=lhsT, rhs=rhs, start=start, stop=stop),
                 reads=[lhsT, rhs], writes=[out])

        def act(out, in_, func, bias=None, scale=None, reads_extra=(), accum_out=None):
            kw = {}
            if accum_out is not None:
                kw["accum_out"] = accum_out
            if bias is not None:
                kw["bias"] = bias
            if scale is not None:
                kw["scale"] = scale
            rd = [in_] + list(reads_extra)
            if bias is not None and not isinstance(bias, float):
                rd.append(bias)
            if scale is not None and not isinstance(scale, float):
                rd.append(scale)
            P.op("act", lambda e: e.activation(out=out, in_=in_, func=func, **kw), reads=rd,
                 writes=[out] if accum_out is None else [out, accum_out])

        def tt(eng, out, in0, in1, op):
            P.op(eng, lambda e: e.tensor_tensor(out=out, in0=in0, in1=in1, op=op), reads=[in0, in1], writes=[out])

        def stt(out, in0, scalar, in1, op0, op1):
            rd = [in0, in1]
            if not isinstance(scalar, float):
                rd.append(scalar)
            P.op("dve", lambda e: e.scalar_tensor_tensor(out=out, in0=in0, scalar=scalar, in1=in1, op0=op0, op1=op1),
                 reads=rd, writes=[out])

        def ts(eng, out, in0, s1, s2, op0, op1=None):
            rd = [in0]
            if not isinstance(s1, float):
                rd.append(s1)
            if s2 is not None and not isinstance(s2, float):
                rd.append(s2)
            if op1 is None:
                P.op(eng, lambda e: e.tensor_scalar(out=out, in0=in0, scalar1=s1, scalar2=None, op0=op0), reads=rd, writes=[out])
            else:
                P.op(eng, lambda e: e.tensor_scalar(out=out, in0=in0, scalar1=s1, scalar2=s2, op0=op0, op1=op1), reads=rd, writes=[out])

        def cp(eng, out, in_):
            if eng == "act":
                P.op(eng, lambda e: e.activation(out=out, in_=in_, func=AF.Copy), reads=[in_], writes=[out])
            else:
                P.op(eng, lambda e: e.tensor_copy(out=out, in_=in_), reads=[in_], writes=[out])

        def recip(out, in_):
            P.op("dve", lambda e: e.reciprocal(out=out, in_=in_), reads=[in_], writes=[out])

        def memset(eng, ap, val):
            P.op(eng, lambda e: e.memset(ap, val), writes=[ap])

        def bcast_free(ap2d, n):
            d = ap2d.ap
            return bass.AP(ap2d.tensor, ap2d.offset, [list(d[0]), [0, n]])

        def setup():
            memset("pool", ones_bf[:, :], 1.0)
            memset("pool", kmean[:, :, :], 0.0)
            memset("pool", kmh[:, :, :], 0.0)
            memset("pool", kml[:, :, :], 0.0)
            dma(ident_f[:, :], ident_d[:, :], writes=[ident_f[:, :]])
            cp("pool", ident_bf[:, :], ident_f[:, :])
            dma(gl[:, :, :], gl_d.rearrange("p (n k) -> p n k", k=8), writes=[gl[:, :, :]])
            dma(cw[:, :, :], cw_d.rearrange("p (j c) -> p j c", c=4), writes=[cw[:, :, :]])
            dma(tbl_bc[:, :], relb_d.rearrange("b h -> (b h)").partition_broadcast(128), writes=[tbl_bc[:, :]])
            dma(sexp[:, :], sinks_d[0, :].partition_broadcast(128), writes=[sexp[:, :]])
            act(sexp[:, :], sexp[:, :], AF.Exp)
            import os as _os
            if not _os.environ.get("SKIPC"):
                dma(tabs[:, :, :, :].rearrange("p a b c -> p (a b c)"), tabs_d[0, :].partition_broadcast(128), writes=[tabs[:, :, :, :]])
            memset("pool", sel[:, :, :], 0.0)
            for half in range(0 if _os.environ.get("SKIPC") else 2):
                sv = stage[0:32, half, 0:2048]
                dma(sv, sel_d[:, half * 2048:(half + 1) * 2048], writes=[sv])
                cp("pool", sel[0:32, half * 16:(half + 1) * 16, :].rearrange("p a b -> p (a b)"), sv)
            tb_aug = af32(0, [8])[0:33]
            ohv = af32(16, [LF])[0:33]
            tbh = af32(16 + 2 * LF, [8, 128])[0:33]
            memset("pool", tb_aug[32:33, :], NEGB)
            dma(tb_aug[0:32, :], relb_d[:, :], writes=[tb_aug[0:32, :]])
            dma(ohv, ohv_d[:, :], writes=[ohv])
            for h in range(8):
                cp("dve", tbh[:, h, :], bcast_free(tb_aug[:, h:h + 1], 128))
            for h in range(8):
                for cch in range(3):
                    mm(ps[cch][:, :], tbh[:, h, :], ohv[:, cch * 512:(cch + 1) * 512], True, True)
                sv = stage[:, h % NSTG, 0:LF]
                for cch in range(3):
                    cp("act" if cch == 1 else "dve", sv[:, cch * 512:(cch + 1) * 512], ps[cch][:, :])
                dma(frep_d[:, h, :], sv, reads=[sv], wk=[("frep", h)])
            for h in range(8):
                s1 = stage[:, h % NSTG, 0:1024]
                src = bass.AP(frep_d.tensor, h * LF + 127, [[8 * LF - 1, 128], [1, 1024]])
                dma(s1, src, writes=[s1], rk=[("frep", h)])
                cp("pool", strips[:, h, :], s1)
                s2 = stage[:, h % NSTG, 1024:1280]
                src2 = bass.AP(frep_d.tensor, h * LF + LM + 127, [[8 * LF - 1, 128], [1, 256]])
                dma(s2, src2, writes=[s2], rk=[("frep", h)])
                cp("pool", swaB[:, h, :], s2)

        def rmsnorm(src, gi, dst, ncols=SUB, nsub=NSUB):
            for sub in range(nsub):
                pn = ps[7]
                for k in range(8):
                    s_ = sq[:, st["sq"] % 2, 0:ncols]
                    st["sq"] += 1
                    act(s_, src(k, sub), AF.Square)
                    mm(pn[:, 0:ncols], ones_bf[:, :], s_, k == 0, k == 7)
                r_ = rt[:, 0, 0:ncols]
                act(r_, pn[:, 0:ncols], AF.Sqrt, bias=eps_ap, scale=1.0 / D)
                rs = r_
                recip(rs, r_)
                for k in range(8):
                    stt(dst(k, sub), src(k, sub), gl[:, gi, k:k + 1], rs, ALU.mult, ALU.mult)

        def xsrc(k, sub):
            return xT[:, k, sub * SUB:(sub + 1) * SUB]

        def hdst(k, sub):
            return hT[:, k, sub * SUB:(sub + 1) * SUB]

        WGU_OFF = 0
        WD_OFF = 8192
        ACT_OFF = WD_OFF + 11264
        SIL_OFF = ACT_OFF + 11264
        assert SIL_OFF + 2048 <= ARENA

        def ffn(wg, wu, wd, l, gi):
            rmsnorm(xsrc, gi, hdst)
            wgu = abf(WGU_OFF, [2, 2, 8, 256])
            wdb = abf(WD_OFF, [11, 1024])
            actb = abf(ACT_OFF, [11, TS])
            sil = af32(SIL_OFF, [2, SUB])
            cnt = 0
            for half in range(2):
                c0 = half * 11
                for pr in range(6):
                    nch = 2 if pr < 5 else 1
                    col = (c0 + 2 * pr) * 128
                    wcols = nch * 128
                    b = pr % 2
                    wload(wg[l].rearrange("(k p) c -> p k c", p=128)[:, :, col:col + wcols], [8, wcols],
                          wgu[:, b, 0, :, 0:wcols], eng="act")
                    wload(wu[l].rearrange("(k p) c -> p k c", p=128)[:, :, col:col + wcols], [8, wcols],
                          wgu[:, b, 1, :, 0:wcols], eng="act")
                    wload(wd[l][(c0 + 2 * pr) * 128:(c0 + 2 * pr + nch) * 128, :].rearrange("(c p) n -> p c n", p=128),
                          [nch, 1024], wdb[:, 2 * pr:2 * pr + nch, :], eng="pool")
                    for ci in range(nch):
                        c = 2 * pr + ci
                        for sub in range(NSUB):
                            pg = ps[cnt % 2]
                            pu = ps[2 + cnt % 2]
                            cnt += 1
                            for k in range(8):
                                mm(pg[:, :], wgu[:, b, 0, k, ci * 128:(ci + 1) * 128], hdst(k, sub), k == 0, k == 7)
                            for k in range(8):
                                mm(pu[:, :], wgu[:, b, 1, k, ci * 128:(ci + 1) * 128], hdst(k, sub), k == 0, k == 7)
                            sl = sil[:, cnt % 2, :]
                            act(sl, pg[:, :], AF.Silu)
                            tt("dve", actb[:, c, sub * SUB:(sub + 1) * SUB], sl, pu[:, :], ALU.mult)
                for m in range(8):
                    for sub in range(NSUB):
                        pd = ps[4 + (m * NSUB + sub) % 2]
                        for c in range(11):
                            mm(pd[:, :], wdb[:, c, m * 128:(m + 1) * 128], actb[:, c, sub * SUB:(sub + 1) * SUB], c == 0, c == 10)
                        xs = xsrc(m, sub)
                        stt(xs, pd[:, :], 0.5, xs, ALU.mult, ALU.add)

        def attn(qT, ktiles, M, N, out_ap, den_add=None):
            n = len(ktiles)
            pO = ps[2 + st["ppb"] % 2]
            pD = ps[4 + st["ppb"] % 2]
            st["ppb"] += 1
            pbs = []

            def s_stage(i):
                t = ktiles[i]
                pS = ps[(0, 1, 6)[i % 3]]
                ex = t.get("extra", [])
                mm(pS[:, 0:N], t["kT"], qT, True, len(ex) == 0)
                for j, (l_, r_) in enumerate(ex):
                    mm(pS[:, 0:N], l_, r_, False, j == len(ex) - 1)
                pb = Pb[:, st["pb"] % 3, 0:N]
                st["pb"] += 1
                act(pb, pS[:, 0:N], AF.Exp, bias=t.get("bias"))
                pbs.append(pb)

            def o_stage(i):
                t = ktiles[i]
                mm(pO[0:M, 0:N], t["v"], pbs[i], i == 0, i == n - 1)
                mm(pD[0:M, 0:N], ones_bf[:, 0:M], pbs[i], i == 0, i == n - 1)

            s_stage(0)
            if n > 1:
                s_stage(1)
            for i in range(n):
                if i + 2 < n:
                    s_stage(i + 2)
                o_stage(i)
            r_ = rD[0:M, 0, 0:N]
            st["rd"] += 1
            if den_add is not None:
                den_add(r_, pD[0:M, 0:N])
                recip(r_, r_)
            else:
                recip(r_, pD[0:M, 0:N])
            po_v = pO[0:M, 0:N]
            if len(out_ap.ap) == 3:
                a_, b_ = out_ap.ap[1][1], out_ap.ap[2][1]
                po_v = bass.AP(po_v.tensor, po_v.offset, [list(po_v.ap[0]), [b_, a_], [1, b_]])
                r_ = bass.AP(r_.tensor, r_.offset, [list(r_.ap[0]), [b_, a_], [1, b_]])
            tt("dve", out_ap, po_v, r_, ALU.mult)

        def proj(wsrc, col0, ncols, M, evac, nk=8, rhs=None, pcols=256):
            rhs = rhs or hdst
            blk = 0
            for pc0 in range(col0, col0 + ncols, pcols):
                pw = min(pcols, col0 + ncols - pc0)
                wp = wp_slot([nk, pw])
                wload(wsrc.rearrange("(k p) c -> p k c", p=128)[:, :, pc0:pc0 + pw], [nk, pw], wp)
                for b in range(pw // M):
                    for sub in range(NSUB):
                        pp = ps[4 + st["ppb"] % 4]
                        st["ppb"] += 1
                        for k in range(nk):
                            mm(pp[0:M, :], wp[:, k, b * M:(b + 1) * M], rhs(k, sub), k == 0, k == nk - 1)
                        evac(blk, sub, pp[0:M, :])
                    blk += 1

        def resid_add(m, sub, pp):
            xs = xsrc(m, sub)
            tt("dve", xs, pp, xs, ALU.add)

        def xa(l, gi):
            rmsnorm(xsrc, gi, hdst)
            qx = abf(0, [4, TS])
            sc = 128 ** -0.5

            def ev_q(blk, sub, pp):
                act(qx[:, blk, sub * SUB:(sub + 1) * SUB], pp, AF.Copy, scale=sc)
            proj(w_xq[l], 0, 512, 128, ev_q)
            for h in range(4):
                for sub in range(NSUB):
                    q = qx[:, h, sub * SUB:(sub + 1) * SUB]
                    kts = [dict(kT=memK[:, l, h, j * 128:(j + 1) * 128], v=memV[:, l, j, h * 128:(h + 1) * 128]) for j in range(2)]
                    attn(q, kts, 128, SUB, q)
            proj(w_xo[l], 0, 1024, 128, resid_add, nk=4, rhs=lambda k, sub: qx[:, k, sub * SUB:(sub + 1) * SUB], pcols=512)

        YA_OFF = 0
        QS_OFF = 4096
        SK_OFF = QS_OFF + 8192
        SV_OFF = SK_OFF + 2304
        VB_OFF = SV_OFF + 1152
        CT_OFF = VB_OFF + 4224
        assert CT_OFF + 6144 <= ARENA

        def even_mixer(stile, gi):
            rmsnorm(xsrc, gi, hdst)
            w = w_evin[0]
            yaT = abf(YA_OFF, [4, TS])
            qsf = abf(QS_OFF, [8, TS])
            skf = abf(SK_OFF, [2, 128 + TS])
            qs = qsf[0:64]
            sk = skf[0:64]
            memset("pool", qsf[64:128, :, :], 0.0)
            memset("pool", skf[64:128, :, :], 0.0)
            sv = abf(SV_OFF, [9, 128])
            vbuf = af32(VB_OFF, [4, 528])
            ctmp = af32(CT_OFF, [3, SUB])
            ctmp2 = af32(CT_OFF + 3072, [3, SUB])
            wsrc = w.rearrange("(k p) c -> p k c", p=128)
            if stile == 0:
                memset("pool", vbuf[:, :, :], 0.0)
            else:
                cp("pool", vbuf[:, :, 0:2], vhist[:, :, :])
                cp("pool", sk[:, :, 0:128], skh[:, :, :])
                cp("pool", sv[:, 0, :], svh[:, :])
            for jp in range(2):
                wps = []
                for part in range(3):
                    wp = wp_slot([8, 256])
                    c0 = part * 512 + jp * 256
                    wload(wsrc[:, :, c0:c0 + 256], [8, 256], wp)
                    wps.append(wp)
                for ji in range(2):
                    j = jp * 2 + ji
                    for sub in range(NSUB):
                        it_ = j * NSUB + sub
                        pB, pC, pU = ((ps[4], ps[5], ps[6]), (ps[0], ps[1], ps[2]))[it_ % 2]
                        ct_ = (ctmp, ctmp2)[it_ % 2]
                        for part, pp in enumerate((pB, pC, pU)):
                            for k in range(8):
                                mm(pp[:, :], wps[part][:, k, ji * 128:(ji + 1) * 128], hdst(k, sub), k == 0, k == 7)
                        u_sb, t1, t2 = ct_[:, 0, :], ct_[:, 1, :], ct_[:, 2, :]
                        act(u_sb, pU[:, :], AF.Copy)
                        tt("dve", vbuf[:, j, 2:2 + SUB], pC[:, :], u_sb, ALU.mult)
                        act(t1, vbuf[:, j, 0:SUB], AF.Copy, scale=cw[:, 0, j:j + 1])
                        stt(t2, vbuf[:, j, 1:1 + SUB], cw[:, 1, j:j + 1], t1, ALU.mult, ALU.add)
                        stt(t1, vbuf[:, j, 2:2 + SUB], cw[:, 2, j:j + 1], t2, ALU.mult, ALU.add)
                        tt("dve", yaT[:, j, sub * SUB:(sub + 1) * SUB], pB[:, :], t1, ALU.mult)
                        cp("act", vbuf[:, j, 0:2], vbuf[:, j, SUB:SUB + 2])
            def ev_q(blk, sub, pp):
                act(qs[:, blk, sub * SUB:(sub + 1) * SUB], pp, AF.Copy, scale=0.125)
            proj(w, 1536, 512, 64, ev_q)
            wp = wp_slot([8, 256])
            wload(wsrc[:, :, 2048:2304], [8, 256], wp)
            for kvh in range(2):
                for sub in range(NSUB):
                    pp = ps[4 + st["ppb"] % 4]
                    st["ppb"] += 1
                    for k in range(8):
                        mm(pp[0:64, :], wp[:, k, kvh * 64:(kvh + 1) * 64], hdst(k, sub), k == 0, k == 7)
                    cp("act", sk[:, kvh, 128 + sub * SUB:128 + (sub + 1) * SUB], pp[0:64, :])
            for tt_ in range(8):
                pp = ps[4 + st["ppb"] % 4]
                st["ppb"] += 1
                for k in range(8):
                    mm(pp[:, 0:128], hT[:, k, tt_ * 128:(tt_ + 1) * 128], wp[:, k, 128:256], k == 0, k == 7)
                cp("act", sv[:, 1 + tt_, :], pp[:, 0:128])
            for n in range(8):
                first = (stile == 0 and n == 0)
                for g in range(2):
                    base = qsf[:, 4 * g, n * 128:(n + 1) * 128]
                    q = bass.AP(base.tensor, base.offset, [list(base.ap[0]), [TS, 4], [1, 128]])
                    kts = []
                    for typ in ((1,) if first else (0, 1)):
                        kcol = n * 128 + typ * 128
                        bsrc = swaB[:, 4 * g, (1 - typ) * 128:(1 - typ) * 128 + 128]
                        brhs = bass.AP(bsrc.tensor, bsrc.offset, [list(bsrc.ap[0]), [256, 4], [1, 128]])
                        kts.append(dict(kT=skf[:, g, kcol:kcol + 128], v=sv[:, n + typ, g * 64:(g + 1) * 64],
                                        extra=[(ident_bf[:, :], brhs)]))

                    def den_add(r_, pD_, g=g):
                        o3 = bass.AP(r_.tensor, r_.offset, [list(r_.ap[0]), [128, 4], [1, 128]])
                        i3 = bass.AP(pD_.tensor, pD_.offset, [list(pD_.ap[0]), [128, 4], [1, 128]])
                        sx = sexp[0:64, 4 * g:4 * g + 4]
                        s3 = bass.AP(sx.tensor, sx.offset, [list(sx.ap[0]), [1, 4], [0, 128]])
                        tt("dve", o3, i3, s3, ALU.add)
                    qo = bass.AP(q.tensor, q.offset, [[q.ap[0][0], 64]] + [list(d_) for d_ in q.ap[1:]])
                    attn(q, kts, 64, 512, qo, den_add=den_add)
            cp("pool", vhist[:, :, :], vbuf[:, :, 0:2])
            cp("pool", skh[:, :, :], sk[:, :, TS:TS + 128])
            cp("pool", svh[:, :], sv[:, 8, :])
            wo = w_evout[0]
            for pc in range(4):
                wpa = wp_slot([4, 256])
                wload(wo[0:512, :].rearrange("(k p) c -> p k c", p=128)[:, :, pc * 256:(pc + 1) * 256], [4, 256], wpa)
                wpb = wp_slot([8, 256], npart=64)
                wload(wo[512:1024, :].rearrange("(h p) c -> p h c", p=64)[:, :, pc * 256:(pc + 1) * 256], [8, 256], wpb, npart=64)
                for b in range(2):
                    m = pc * 2 + b
                    for sub in range(NSUB):
                        pp = ps[4 + st["ppb"] % 4]
                        st["ppb"] += 1
                        for j in range(4):
                            mm(pp[:, :], wpa[:, j, b * 128:(b + 1) * 128], yaT[:, j, sub * SUB:(sub + 1) * SUB], j == 0, False)
                        for h in range(8):
                            wpbf = bass.AP(wpb.tensor, wpb.offset, [[wpb.ap[0][0], 128]] + [list(d_) for d_ in wpb.ap[1:]])
                            mm(pp[:, :], wpbf[:, h, b * 128:(b + 1) * 128], qsf[:, h, sub * SUB:(sub + 1) * SUB], False, h == 7)
                        resid_add(m, sub, pp[:, :])

        QM_OFF = 0
        KB_OFF = 8192
        VB2_OFF = KB_OFF + 4096
        MT_OFF = VB2_OFF + 4096
        TMP_OFF = MT_OFF + 4096
        GM_OFF = TMP_OFF + 1024
        KB2_OFF = GM_OFF + 1024
        VB3_OFF = KB2_OFF + 4096
        assert VB3_OFF + 4096 <= ARENA

        def odd_mixer(stile, gi):
            rmsnorm(xsrc, gi, hdst)
            t0 = stile * TS
            w = w_odin[0]
            import os as _os
            if _os.environ.get("ODD_WSWAP"):
                w = w_evin[0]
            wsrc = w.rearrange("(k p) c -> p k c", p=128)
            qm = abf(QM_OFF, [8, TS])
            kbs = [abf(KB_OFF, [S]), abf(KB2_OFF, [S])]
            vbs = [abf(VB2_OFF, [32, 128]), abf(VB3_OFF, [32, 128])]
            mTf = abf(MT_OFF, [4, TS])
            mT = mTf[0:32]
            memset("pool", mTf[32:64, :, :], 0.0)
            memset("pool", mTf[64:128, :, :], 0.0)
            tmpkv = abf(TMP_OFF, [2, SUB])
            gm = af32(GM_OFF, [8, 16])
            top8 = af32(GM_OFF + 256, [8, 8])
            mb = af32(GM_OFF + 512, [8, 16])
            sc = 128 ** -0.5

            def ev_q(blk, sub, pp):
                act(qm[:, blk, sub * SUB:(sub + 1) * SUB], pp, AF.Copy, scale=sc)
            import os as _os
            _lvl = int(_os.environ.get("ODD_LVL", "9"))
            if _lvl <= 0:
                return
            proj(w, 0, 1024, 128, ev_q)
            if _lvl <= 1:
                return

            def ev_k(blk, sub, pp):
                tk = tmpkv[:, (blk * NSUB + sub) % 2, :]
                for hb in range(2):
                    act(tk[:, hb * 256:(hb + 1) * 256], pp[:, hb * 256:(hb + 1) * 256], AF.Copy,
                        accum_out=ksum[:, blk, 2 * sub + hb:2 * sub + hb + 1])
                tok = t0 + sub * SUB
                dma(kc_d[blk, :, tok:tok + SUB], tk, reads=[tk], wk=[("kc", blk, stile)])
            proj(w, 1024, 512, 128, ev_k)
            b0 = t0 // 256
            ts("dve", kmean[:, :, b0:b0 + 4], ksum[:, :, :], 1.0 / 256, None, ALU.mult)
            cp("dve", kmh[:, :, b0:b0 + 4], kmean[:, :, b0:b0 + 4])
            tt("dve", kml[:, :, b0:b0 + 4], kmean[:, :, b0:b0 + 4], kmh[:, :, b0:b0 + 4], ALU.subtract)
            if _lvl <= 2:
                return
            for pc in range(2):
                wp = wp_slot([8, 256])
                wload(wsrc[:, :, 1536 + pc * 256:1536 + (pc + 1) * 256], [8, 256], wp)
                for tt_ in range(8):
                    pp = ps[4 + st["ppb"] % 4]
                    st["ppb"] += 1
                    for k in range(8):
                        mm(pp[:, 0:256], hT[:, k, tt_ * 128:(tt_ + 1) * 128], wp[:, k, :], k == 0, k == 7)
                    tv = tmpkv[:, tt_ % 2, 0:256]
                    cp("act", tv, pp[:, 0:256])
                    tok = t0 + tt_ * 128
                    dst = vc_d[2 * pc:2 * pc + 2, tok:tok + 128, :].rearrange("v t d -> t v d")
                    import os as _os
                    if not _os.environ.get("ODD_NODMA") and not _os.environ.get("ODD_NOVDMA"):
                        for v_ in range(2):
                            dma(vc_d[2 * pc + v_, tok:tok + 128, :], tv[:, v_ * 128:(v_ + 1) * 128], reads=[tv[:, v_ * 128:(v_ + 1) * 128]],
                                wk=[("vc", 2 * pc + v_, stile)])
            import os as _os
            _stop = int(_os.environ.get("ODD_STOP", "9"))
            if _stop <= 1:
                return
            for qt in range(8):
                own = (t0 + qt * 128) // 256
                pG = ps[6]
                for h in range(8):
                    kv = h // 2
                    mm(pG[:, h * 16:(h + 1) * 16], qm[:, h, qt * 128:(qt + 1) * 128], kmh[:, kv, :], True, False)
                    mm(pG[:, h * 16:(h + 1) * 16], qm[:, h, qt * 128:(qt + 1) * 128], kml[:, kv, :], False, True)
                negr = tabs[:, 0, own, :]
                neg3 = bass.AP(negr.tensor, negr.offset, [list(negr.ap[0]), [0, 8], [1, 16]])
                pg3 = pG[:, 0:128].rearrange("p (h b) -> p h b", b=16)
                tt("dve", gm[:, :, :], pg3, neg3, ALU.add)
                for h in range(8):
                    P.op("dve", lambda e, h=h: e.max(out=top8[:, h, :], in_=gm[:, h, :]), reads=[gm[:, h, :]], writes=[top8[:, h, :]])
                thr = top8[:, :, 2:3]
                thr3 = bass.AP(thr.tensor, thr.offset, [list(thr.ap[0]), [8, 8], [0, 16]])
                tt("dve", mb[:, :, :], gm[:, :, :], thr3, ALU.is_ge)
                ar = tabs[:, 1, own, :]
                a3 = bass.AP(ar.tensor, ar.offset, [list(ar.ap[0]), [0, 8], [1, 16]])
                br = tabs[:, 2, own, :]
                b3 = bass.AP(br.tensor, br.offset, [list(br.ap[0]), [0, 8], [1, 16]])
                tt("dve", mb[:, :, :], mb[:, :, :], a3, ALU.mult)
                tt("dve", mb[:, :, :], mb[:, :, :], b3, ALU.add)
                pT = ps7b
                mbf = mb16[:, :]
                cp("dve", mbf, mb[:, :, :].rearrange("p h b -> p (h b)"))
                for pr in range(4):
                    P.op("pe", lambda e, pr=pr: e.transpose(out=pT[0:32, pr * 128:(pr + 1) * 128], in_=mbf[:, pr * 32:(pr + 1) * 32], identity=ident_bf[:, :]),
                         reads=[mbf[:, pr * 32:(pr + 1) * 32], ident_bf[:, :]], writes=[pT[0:32, pr * 128:(pr + 1) * 128]])
                cp("act", mT[:, :, qt * 128:(qt + 1) * 128], pT[0:32, 0:512].rearrange("p (a q) -> p a q", q=128))
            if _stop <= 2:
                return
            for qsub in range(NSUB):
                nkt = (t0 + (qsub + 1) * SUB) // 128
                for kv in range(4):
                    kb = kbs[(qsub * 4 + kv) % 2]
                    vb = vbs[(qsub * 4 + kv) % 2]
                    ntok = nkt * 128
                    rkeys = [("kc", kv, s_) for s_ in range(stile + 1)]
                    dma(kb[:, 0:ntok], kc_d[kv, :, 0:ntok], writes=[kb[:, 0:ntok]], rk=rkeys)
                    rkeys = [("vc", kv, s_) for s_ in range(stile + 1)]
                    for c8 in range(0, nkt, 4):
                        c9 = min(nkt, c8 + 4)
                        dma(vb[:, c8:c9, :], vc_d[kv, c8 * 128:c9 * 128, :].rearrange("(t p) d -> p t d", p=128), writes=[vb[:, c8:c9, :]], rk=rkeys)
                    for h in (2 * kv, 2 * kv + 1):
                        q = qm[:, h, qsub * SUB:(qsub + 1) * SUB]
                        kts = []
                        for kt in range(nkt):
                            a = kt - (nkt - 4)
                            blkid = kt // 2
                            ex = [(sel[:, (h % 2) * 16 + blkid, :], mTf[:, h // 2, qsub * SUB:(qsub + 1) * SUB])]
                            bias = None
                            if a >= -1:
                                c0 = (3 - a) * 128
                                ex.append((ident_bf[:, :], strips[:, h, c0:c0 + SUB]))
                            else:
                                bias = tbl_bc[:, 31 * 8 + h:31 * 8 + h + 1]
                            kts.append(dict(kT=kb[:, kt * 128:(kt + 1) * 128], v=vb[:, kt, :], extra=ex, bias=bias))
                        attn(q, kts, 128, SUB, q)
            proj(w_odout[0], 0, 1024, 128, resid_add, rhs=lambda k, sub: qm[:, k, sub * SUB:(sub + 1) * SUB])

        def prologue():
            memT = af32(0, [8, MEM])
            memn = abf(4096, [8, MEM])
            dma(memT, memTd.rearrange("(k p) m -> p k m", p=128), writes=[memT])
            rmsnorm(lambda k, sub: memT[:, k, :], 9, lambda k, sub: memn[:, k, :], ncols=MEM, nsub=1)
            for l in range(2):
                wsrc = w_xkv[l].rearrange("(k p) c -> p k c", p=128)
                for pc in range(2):
                    wp = wp_slot([8, 256])
                    wload(wsrc[:, :, pc * 256:(pc + 1) * 256], [8, 256], wp)
                    for b in range(2):
                        h = pc * 2 + b
                        pp = ps[4 + st["ppb"] % 4]
                        st["ppb"] += 1
                        for k in range(8):
                            mm(pp[:, 0:MEM], wp[:, k, b * 128:(b + 1) * 128], memn[:, k, :], k == 0, k == 7)
                        cp("act", memK[:, l, h, :], pp[:, 0:MEM])
                for pc in range(2):
                    wp = wp_slot([8, 256])
                    wload(wsrc[:, :, 512 + pc * 256:512 + (pc + 1) * 256], [8, 256], wp)
                    for j in range(2):
                        pp = ps[4 + st["ppb"] % 4]
                        st["ppb"] += 1
                        for k in range(8):
                            mm(pp[:, 0:256], memn[:, k, j * 128:(j + 1) * 128], wp[:, k, :], k == 0, k == 7)
                        cp("act", memV[:, l, j, pc * 256:(pc + 1) * 256], pp[:, 0:256])

        eps_ap = gl[:, 9, 0:1]
        eps_t = epsc[:, 0:1]
        eps_ap = eps_t
        memset("pool", epsc[:, :], EPS)
        setup()
        prologue()
        import os as _os2
        for stile in range(int(_os2.environ.get('NST_RUN', NST))):
            t0 = stile * TS
            st["stile"] = stile
            st["pid"] = 0
            nst_run = int(_os2.environ.get('NST_RUN', NST))
            xnext = af32(16384, [8, TS])
            if stile == 0 or nph < 9:
                dma(xT[:, :, :], xTd.rearrange("(k p) t -> p k t", p=128)[:, :, t0:t0 + TS], writes=[xT[:, :, :]])
            else:
                for k_ in range(8):
                    cp(("act", "dve")[k_ % 2], xT[:, k_, :], xnext[:, k_, :])
            ph = 0
            for l in range(2):
                if ph < nph:
                    ffn(w_f1g, w_f1u, w_f1d, l, 4 * l + 0)
                ph += 1
                if ph < nph:
                    if l == 0:
                        even_mixer(stile, 4 * l + 1)
                    else:
                        odd_mixer(stile, 4 * l + 1)
                ph += 1
                if ph < nph:
                    xa(l, 4 * l + 2)
                ph += 1
                if ph < nph:
                    ffn(w_f2g, w_f2u, w_f2d, l, 4 * l + 3)
                ph += 1
            if nph >= 9:
                ofin = af32(0, [8, SUB])
                if stile + 1 < nst_run:
                    dma(xnext, xTd.rearrange("(k p) t -> p k t", p=128)[:, :, t0 + TS:t0 + 2 * TS], writes=[xnext])
                for sub in range(NSUB):
                    rmsnorm(lambda k, s_: xsrc(k, sub), 8, lambda k, s_: ofin[:, k, :], nsub=1)
                    dma(outTd.rearrange("(k p) t -> p k t", p=128)[:, :, t0 + sub * SUB:t0 + (sub + 1) * SUB], ofin, reads=[ofin])
            else:
                dma(outTd.rearrange("(k p) t -> p k t", p=128)[:, :, t0:t0 + TS], xT[:, :, :], reads=[xT[:, :, :]])

    epsc = sb("epsc", [128, 8], F32)
    Pd = Prog(dry=True)
    record(Pd, None)
    P = Prog(dry=False)
    record(P, list(requests))
    P.analyze(NDS)
    with nc.Block() as block:
        P.emit(nc, sems, dsems, block)
    es.close()
    return nc, len(P.ops)


_CONSTS = None


def make_in_maps(inputs):
    global _CONSTS
    if _CONSTS is None:
        _CONSTS = host_consts()
    c = _CONSTS
    f = lambda a: np.ascontiguousarray(np.asarray(a, dtype=np.float32))
    x = f(inputs["x"])
    mem = f(inputs["mem"])
    G = np.stack([f(inputs["ffn1_norm"])[0], f(inputs["mix_norm"])[0], f(inputs["xa_norm"])[0], f(inputs["ffn2_norm"])[0],
                  f(inputs["ffn1_norm"])[1], f(inputs["mix_norm"])[1], f(inputs["xa_norm"])[1], f(inputs["ffn2_norm"])[1],
                  f(inputs["final_norm"]), f(inputs["mem_norm"])], axis=0)
    gl = np.ascontiguousarray(G.reshape(10, 8, 128).transpose(2, 0, 1).reshape(128, 80))
    cwv = f(inputs["ev_conv_w"])[0]
    cw = np.ascontiguousarray(cwv.reshape(3, 4, 128).transpose(2, 0, 1).reshape(128, 12))
    cparts = {"gl": gl, "cw": cw, "sinks": f(inputs["ev_sinks"]).reshape(-1), "relb": f(inputs["rel_bias"]),
              "ident": c["ident"], "ohv": c["ohv"], "sel": c["sel"], "tabs": c["tabs"]}
    cpack = np.concatenate([np.asarray(cparts[n], np.float32).reshape(-1) for n, _ in CSIZES])
    assert cpack.size == CTOT
    wpack = np.concatenate([f(inputs[n]).reshape(-1) for n, _ in WSHAPES])
    assert wpack.size == WTOT
    shared = {"cpack": cpack, "wpack": wpack}
    maps = []
    for b in range(NCORES):
        m = dict(shared)
        m["xT"] = np.ascontiguousarray(x[b].T)
        m["memT"] = np.ascontiguousarray(mem[b].T)
        maps.append(m)
    return maps


_NC_CACHE = {}


def kernel(**inputs):
    if "nc" not in _NC_CACHE:
        _NC_CACHE["nc"] = build(9)[0]
    nc = _NC_CACHE["nc"]
    maps = make_in_maps(inputs)
    res = run_bass_kernel_spmd(nc, maps, core_ids=list(range(NCORES)))
    out = np.stack([np.ascontiguousarray(np.asarray(r["outT"]).T) for r in res.results], axis=0)
    return out.astype(np.float32)
```
